# Optimizing a Trainium2 kernel written in Bass

```python
import jax, jax.numpy as jnp
from jax import lax
import numpy as np

D_MODEL = 1024
BATCH = 32
SEQ = 2048
DEPTH = 1
DEC_BATCH = 16
DEC_SEQ = 16
PAST_LEN = 4096

CHUNK = 64
D_MIX = D_MODEL
A_WIDTH = D_MIX // 2
B_WIDTH = D_MIX - A_WIDTH
GMLP_CHUNK = 128
A_GROUPS = 4
A_GROUP_DIM = A_WIDTH // A_GROUPS
B_HEADS = 4
B_KEY_DIM = B_WIDTH // B_HEADS
B_VAL_DIM = B_WIDTH // B_HEADS
B_FDIM = B_HEADS * B_KEY_DIM
HGRN_BLOCK = 32
EPS = 1e-6
IN_WIDTH = 3 * A_WIDTH + 2 * B_FDIM + 2 * B_WIDTH

kernel_name = 'hybrid_gmlp_hgrn2_stream_step'


def rmsnorm(x, g):
    xf = x.astype(jnp.float32)
    y = xf * lax.rsqrt(jnp.mean(xf * xf, axis=-1, keepdims=True) + EPS)
    return (y * g).astype(x.dtype)


def layernorm(x, g, b):
    xf = x.astype(jnp.float32)
    mu = jnp.mean(xf, axis=-1, keepdims=True)
    var = jnp.mean(jnp.square(xf - mu), axis=-1, keepdims=True)
    return ((xf - mu) * lax.rsqrt(var + EPS) * g + b).astype(x.dtype)


def hgrn2_chunkwise(q, k, v, log_f, S0, block):
    Bsz, T, H, DK = q.shape
    DV = v.shape[-1]
    N = T // block
    r = lambda t: t.reshape(Bsz, N, block, H, t.shape[-1])
    q, k, v, log_f = r(q), r(k), r(v), r(log_f)
    b = jnp.cumsum(log_f, axis=2)
    b_last = b[:, :, -1]
    q_dec = q * jnp.exp(b)
    k_inv = k * jnp.exp(-b)
    k_end = k * jnp.exp(b_last[:, :, None] - b)
    causal = jnp.tril(jnp.ones((block, block), dtype=bool))
    att = jnp.einsum('bnthd,bnshd->bnhts', q_dec, k_inv)
    att = jnp.where(causal, att, 0.0)
    o_intra = jnp.einsum('bnhts,bnshv->bnthv', att, v)
    dS = jnp.einsum('bnshd,bnshv->nbhdv', k_end, v)
    decay = jnp.moveaxis(jnp.exp(b_last), 1, 0)

    def step(S, inp):
        dec, ds = inp
        return dec[..., None] * S + ds, S

    S_fin, S_start = lax.scan(step, S0, (decay, dS))
    o_inter = jnp.einsum('bnthd,nbhdv->bnthv', q_dec, S_start)
    o = (o_intra + o_inter).reshape(Bsz, T, H, DV)
    return o, S_fin


def mixer_layer(x, c, S0, lb, norm_g, w_ada, b_ada, w_in, ln_v_g, ln_v_b, w_sp, b_sp,
                gnorm_g, w_out):
    Bsz, L, _ = x.shape
    mod = jax.nn.silu(c) @ w_ada + b_ada
    shift, scale, gate = jnp.split(mod, 3, axis=-1)
    h = rmsnorm(x, norm_g) * (1.0 + scale[:, None]) + shift[:, None]
    z = h @ w_in
    o1 = A_WIDTH; o2 = o1 + A_WIDTH; o3 = o2 + A_WIDTH
    o4 = o3 + B_FDIM; o5 = o4 + B_FDIM; o6 = o5 + B_WIDTH
    u, v, ga = z[..., :o1], z[..., o1:o2], z[..., o2:o3]
    qb, fb, ib, gb = z[..., o3:o4], z[..., o4:o5], z[..., o5:o6], z[..., o6:]

    v = layernorm(v, ln_v_g, ln_v_b)
    P = min(L, GMLP_CHUNK)
    N = L // P
    pos = jnp.arange(P)
    mask = (pos[None, :] // CHUNK) <= (pos[:, None] // CHUNK)
    Wm = jnp.where(mask[None], w_sp[:, :P, :P], 0.0)
    vr = v.reshape(Bsz, N, P, A_GROUPS, A_GROUP_DIM)
    sp = jnp.einsum('gij,bnjgc->bnigc', Wm, vr) + jnp.transpose(b_sp[:, :P])[None, None, :, :, None]
    a_out = u * sp.reshape(Bsz, L, A_WIDTH).astype(u.dtype) * jax.nn.silu(ga)

    qf = jax.nn.silu(qb.astype(jnp.float32)).reshape(Bsz, L, B_HEADS, B_KEY_DIM)
    fg = lb + (1.0 - lb) * jax.nn.sigmoid(fb.astype(jnp.float32))
    kf = (1.0 - fg).reshape(Bsz, L, B_HEADS, B_KEY_DIM)
    log_f = jnp.log(fg).reshape(Bsz, L, B_HEADS, B_KEY_DIM)
    vf = ib.astype(jnp.float32).reshape(Bsz, L, B_HEADS, B_VAL_DIM)
    block = HGRN_BLOCK if L % HGRN_BLOCK == 0 else L
    o, S_fin = hgrn2_chunkwise(qf, kf, log_f, vf, S0, block) if False else hgrn2_chunkwise(qf, kf, vf, log_f, S0, block)
    o = rmsnorm(o, gnorm_g.reshape(B_HEADS, B_VAL_DIM)).reshape(Bsz, L, B_WIDTH)
    b_out = o.astype(x.dtype) * jax.nn.silu(gb)

    out = jnp.concatenate([a_out, b_out], axis=-1) @ w_out
    x = x + gate[:, None] * out
    return x, S_fin, v


def setup_inputs(seed: int = 0) -> dict:
    key = jax.random.key(seed)
    ks = jax.random.split(key, 20)
    f32 = jnp.float32
    nrm = lambda k, s: jax.random.normal(k, s, f32)
    return {
        'x_prompt': nrm(ks[0], (BATCH, SEQ, D_MODEL)),
        'x_sample': nrm(ks[1], (DEC_BATCH, DEC_SEQ, D_MODEL)),
        'c_prompt': nrm(ks[2], (BATCH, D_MODEL)),
        'c_sample': nrm(ks[3], (DEC_BATCH, D_MODEL)),
        'state_hgrn': 0.5 * nrm(ks[4], (DEPTH, DEC_BATCH, B_HEADS, B_KEY_DIM, B_VAL_DIM)),
        'norm_g': 1.0 + 0.02 * nrm(ks[5], (DEPTH, D_MODEL)),
        'w_ada': 0.5 * D_MODEL ** -0.5 * nrm(ks[6], (DEPTH, D_MODEL, 3 * D_MODEL)),
        'b_ada': 0.02 * nrm(ks[7], (DEPTH, 3 * D_MODEL)),
        'w_in': D_MODEL ** -0.5 * nrm(ks[8], (DEPTH, D_MODEL, IN_WIDTH)),
        'ln_v_g': 1.0 + 0.02 * nrm(ks[9], (DEPTH, A_WIDTH)),
        'ln_v_b': 0.02 * nrm(ks[10], (DEPTH, A_WIDTH)),
        'w_sp': GMLP_CHUNK ** -0.5 * nrm(ks[11], (DEPTH, A_GROUPS, GMLP_CHUNK, GMLP_CHUNK)),
        'b_sp': 1.0 + 0.1 * nrm(ks[12], (DEPTH, A_GROUPS, GMLP_CHUNK)),
        'lb_logits': 0.1 * nrm(ks[13], (DEPTH + 1, B_FDIM)),
        'gnorm_g': 1.0 + 0.02 * nrm(ks[14], (DEPTH, B_WIDTH)),
        'w_out': D_MIX ** -0.5 * nrm(ks[15], (DEPTH, D_MIX, D_MODEL)),
        'g_final': 1.0 + 0.02 * nrm(ks[16], (D_MODEL,)),
        'w_ada_f': 0.5 * D_MODEL ** -0.5 * nrm(ks[17], (D_MODEL, 2 * D_MODEL)),
        'b_ada_f': 0.02 * nrm(ks[18], (2 * D_MODEL,)),
    }


def final_norm(x, c, g_final, w_ada_f, b_ada_f):
    mod = jax.nn.silu(c) @ w_ada_f + b_ada_f
    shift, scale = jnp.split(mod, 2, axis=-1)
    return rmsnorm(x, g_final) * (1.0 + scale[:, None]) + shift[:, None]


def reference(x_prompt, x_sample, c_prompt, c_sample, state_hgrn, norm_g, w_ada, b_ada,
              w_in, ln_v_g, ln_v_b, w_sp, b_sp, lb_logits, gnorm_g, w_out, g_final,
              w_ada_f, b_ada_f):
    lower = jnp.cumsum(jax.nn.softmax(lb_logits.astype(jnp.float32), axis=0), axis=0)
    xp, xs = x_prompt, x_sample
    Sp_list, Ss_list, vs_list = [], [], []
    for l in range(DEPTH):
        S0p = jnp.zeros((xp.shape[0], B_HEADS, B_KEY_DIM, B_VAL_DIM), jnp.float32)
        xp, Sp, _ = mixer_layer(xp, c_prompt, S0p, lower[l], norm_g[l], w_ada[l], b_ada[l],
                                w_in[l], ln_v_g[l], ln_v_b[l], w_sp[l], b_sp[l],
                                gnorm_g[l], w_out[l])
        xs, Ss, vs = mixer_layer(xs, c_sample, state_hgrn[l].astype(jnp.float32), lower[l],
                                 norm_g[l], w_ada[l], b_ada[l], w_in[l], ln_v_g[l],
                                 ln_v_b[l], w_sp[l], b_sp[l], gnorm_g[l], w_out[l])
        Sp_list.append(Sp)
        Ss_list.append(Ss)
        vs_list.append(vs)
    y_prompt = final_norm(xp, c_prompt, g_final, w_ada_f, b_ada_f)
    y_sample = final_norm(xs, c_sample, g_final, w_ada_f, b_ada_f)
    state_hgrn_prompt = jnp.stack(Sp_list)
    state_hgrn_sample = jnp.stack(Ss_list)
    state_gmlp_v_sample = jnp.stack(vs_list)
    return (y_prompt, y_sample, state_hgrn_prompt, state_hgrn_sample, state_gmlp_v_sample)
```

```python
import types
import numpy as np
from contextlib import ExitStack
import concourse.bass as bass
import concourse.mybir as mybir
from concourse.bass_utils import run_bass_kernel_spmd

F32, BF16 = mybir.dt.float32, mybir.dt.bfloat16
AF = mybir.ActivationFunctionType
ALU = mybir.AluOpType

D = 1024
NSEQ = 4
L = 2048
NSMP = 2
LS = 16
EPS = 1e-6
INW = 3584
OU, OV, OGA, OQ, OF, OI, OGB = 0, 512, 1024, 1536, 2048, 2560, 3072


class Buf:
    __slots__ = ("name", "w", "r", "dsem", "dcnt", "started")

    def __init__(self, name):
        self.name = name
        self.w = None
        self.r = {}
        self.dsem = None
        self.dcnt = 0
        self.started = set()


class Eng:
    def __init__(self, name, h, sem):
        self.name, self.h, self.sem = name, h, sem
        self.cnt = 0
        self.waited = {}


def build(stop=99, dbg=None, sub=99):
    nc = bass.Bass("TRN2", target_bir_lowering=False, dynamic_dma_scratch_size=512)
    es = ExitStack()

    def din(name, shape):
        return nc.dram_tensor(name, list(shape), F32, kind="ExternalInput").ap()

    def dout(name, shape):
        return nc.dram_tensor(name, list(shape), F32, kind="ExternalOutput").ap()

    xp = din("xp", [NSEQ * L, D]); xs = din("xs", [NSMP * LS, D]); cT = din("cT", [D, 6])
    s0 = din("s0", [NSMP, 4, 128, 128]); ng = din("ng", [128, 8])
    w_ada = din("w_ada", [D, 3 * D]); b_ada = din("b_ada", [1, 3 * D]); w_in = din("w_in", [D, INW])
    lnvg = din("lnvg", [128, 4]); lnvg_row = din("lnvg_row", [1, 512]); lnvb_row = din("lnvb_row", [1, 512])
    wspT = din("wspT", [4, 128, 128]); bsp = din("bsp", [1, 4, 128]); lbl = din("lbl", [128, 2, 4])
    gn = din("gn", [128, 4]); w_out = din("w_out", [D, D]); gfin = din("gfin", [1, D])
    w_adaf = din("w_adaf", [D, 2 * D]); b_adaf = din("b_adaf", [1, 2 * D])
    ident = din("ident", [128, 128]); maskp = din("maskp", [128, 128]); masks = din("masks", [64, 64])
    cmask = din("cmask", [128, 128]); rmask = din("rmask", [128, 4])
    modscr = nc.dram_tensor("modscr", [6, 3, D], F32, kind="Internal").ap()
    yp = dout("yp", [NSEQ * L, D]); ys = dout("ys", [NSMP * LS, D])
    spo = dout("spo", [NSEQ, 4, 128, 128]); sso = dout("sso", [NSMP, 4, 128, 128]); vso = dout("vso", [NSMP * LS, 512])

    def sb(name, shape, dt=F32):
        return es.enter_context(nc.sbuf_tensor(name, list(shape), dt))

    sempool = [es.enter_context(nc.semaphore("s%d" % i)) for i in range(96)]
    semi = [0]

    def newsem():
        s = sempool[semi[0]]; semi[0] += 1
        return s

    PE = Eng("pe", nc.tensor, newsem()); ACT = Eng("act", nc.scalar, newsem())
    DVE = Eng("dve", nc.vector, newsem()); POOL = Eng("pool", nc.gpsimd, newsem())
    SP = Eng("sp", nc.sync, newsem())

    def _need(reads, writes):
        need = {}

        def add(ev):
            if ev is None:
                return
            s, v = ev
            if need.get(s.num, (s, 0))[1] < v:
                need[s.num] = (s, v)
        for b in reads:
            add(b.w)
        for b in writes:
            add(b.w)
            for ev in b.r.values():
                add(ev)
        return need

    def _waits(eng, reads, writes):
        waits = []
        for num, (s, v) in _need(reads, writes).items():
            if eng.waited.get(num, 0) >= v:
                continue
            if eng is PE and s is PE.sem:
                continue
            if s is PE.sem and eng is not PE:
                assert v <= PE.cnt, "wait on unissued PE increment"
            waits.append((s, v)); eng.waited[num] = v
        return waits

    def _record(ev, reads, writes):
        s, v = ev
        for b in reads:
            old = b.r.get(s.num)
            if old is None or old[1] < v:
                b.r[s.num] = ev
        for b in writes:
            b.w = ev; b.r = {}

    prog = {"pe": [], "act": [], "dve": [], "pool": [], "sp": []}

    def _snap(fn):
        if not fn.__closure__:
            return fn
        cells = []
        for c in fn.__closure__:
            try:
                cells.append(types.CellType(c.cell_contents))
            except ValueError:
                cells.append(c)
        return types.FunctionType(fn.__code__, fn.__globals__, fn.__name__, fn.__defaults__, tuple(cells))

    def op(eng, fn, reads=(), writes=(), inc=True, multi=False):
        fn = _snap(fn)
        waits = _waits(eng, reads, writes)
        emb = None
        if waits and not multi:
            emb = waits[0]; waits = waits[1:]
        if inc:
            eng.cnt += 1
            ev = (eng.sem, eng.cnt)
        else:
            ev = (eng.sem, eng.cnt + 1)
        _record(ev, reads, writes)

        def thunk(h):
            for (s, v) in waits:
                h.wait_ge(s, v)
            ins = fn()
            if emb is not None:
                ins._wait_ge(emb[0], emb[1])
            if inc:
                ins.then_inc(eng.sem, 1)
        prog[eng.name].append(thunk)

    def dma(out_ap, in_ap, reads=(), writes=(), dbuf=None, q=None, **kw):
        q = q or SP
        waits = _waits(q, reads, writes)
        if dbuf.dsem is None:
            dbuf.dsem = newsem()
        dsem = dbuf.dsem
        dbuf.dcnt += 16
        _record((dsem, dbuf.dcnt), reads, writes)

        def thunk(h):
            for (s, v) in waits:
                h.wait_ge(s, v)
            h.dma_start(out=out_ap, in_=in_ap, **kw).then_inc(dsem, 16)
        prog[q.name].append(thunk)

    banks = []
    for i in range(8):
        t = es.enter_context(nc.psum_tensor("bank%d" % i, [128, 512], F32))
        banks.append((t, Buf("bank%d" % i)))
    bi_ = [0]

    pinned = set()

    def newbank():
        while True:
            idx = bi_[0] % 8; bi_[0] += 1
            if idx not in pinned:
                break
        t, b = banks[idx]
        b.started = set()
        return t, b, idx

    def mm(bank, out, lhsT, rhs, reads, last=False, p0=0, npart=128, tp=None):
        bb = bank[1]
        quads = set(range(p0 // 32, (p0 + npart - 1) // 32 + 1))
        if quads.isdisjoint(bb.started):
            start = True; bb.started |= quads
        else:
            assert quads <= bb.started, "psum quadrant start conflict"
            start = False
        kw = {} if tp is None else {"tile_position": tp}
        op(PE, lambda: nc.tensor.matmul(out, lhsT, rhs, start=start, stop=True, skip_group_check=True, **kw),
           reads=reads, writes=[bb], inc=last)

    def tr(bank, out, in_, idn, reads, last=False):
        bb = bank[1]
        op(PE, lambda: nc.tensor.transpose(out, in_, idn), reads=reads, writes=[bb], inc=last)

    Win = sb("Win", [128, 8, INW], BF16); bWin = Buf("Win")
    WOG = sb("WOG", [128, 8, D], BF16); bWOG = Buf("WOG")
    STG = sb("STG", [128, 2, 1024]); bSTG = [Buf("stg%d" % i) for i in range(2)]
    X = sb("X", [128, 2, 4, D]); bX = [[Buf("x%d_%d" % (s, t)) for t in range(4)] for s in range(2)]
    XN = sb("XN", [128, 4, D]); bXN = [Buf("xn%d" % t) for t in range(4)]
    HT = sb("HT", [128, 8, 512], BF16); bHT = [Buf("ht%d" % k) for k in range(8)]
    JUNK = sb("JUNK", [128, D], BF16)
    VH = sb("VH", [128, 4, 512], BF16); bVH = [Buf("vh%d" % t) for t in range(4)]
    SG = sb("SG", [128, 2, 512]); bSG = [Buf("sg0"), Buf("sg1")]
    SPb = sb("SPb", [128, 512]); bSPb = Buf("spb")
    MIX = sb("MIX", [128, 8, 512], BF16); bMIX = [Buf("mix%d" % k) for k in range(8)]
    IB = sb("IB", [128, 4, 512], BF16); bIB = [Buf("ib%d" % t) for t in range(4)]
    TH = [XN[:, h // 2, (h % 2) * 512:(h % 2 + 1) * 512] for h in range(4)]
    Q = [XN[:, 2 + h // 2, (h % 2) * 512:(h % 2 + 1) * 512] for h in range(4)]
    bTH = [Buf("th%d" % h) for h in range(4)]; bQ = [Buf("q%d" % h) for h in range(4)]
    bXNl = [[bXN[0], bTH[0], bTH[1]], [bXN[1], bTH[2], bTH[3]], [bXN[2], bQ[0], bQ[1]], [bXN[3], bQ[2], bQ[3]]]
    BM = sb("BM", [128, 576]); NHALF = sb("NHALF", [128, 4])
    E = sb("E", [128, 2, 512]); bE = [Buf("e0"), Buf("e1")]
    SGB = sb("SGB", [128, 4, 512], BF16); bSGB = [Buf("sgb%d" % i) for i in range(4)]
    KI = sb("KI", [128, 4, 512], BF16); bKI = [Buf("ki%d" % i) for i in range(4)]
    QD = sb("QD", [128, 4, 512], BF16); bQD = [Buf("qd%d" % i) for i in range(4)]
    KT = sb("KT", [128, 4, 2, 512], BF16); bKT = [Buf("kt%d" % i) for i in range(4)]
    ATT = sb("ATT", [128, 4, 512], BF16); bATT = [Buf("att%d" % i) for i in range(4)]
    SBF = sb("SBF", [128, 8, 4, 128], BF16); bSBF = [Buf("sbf%d" % i) for i in range(8)]
    S = {"p": sb("Sp", [128, 4, 128])[:]}
    bS = {"p": Buf("Sp")}
    STMP = sb("STMP", [128, 4, 128]); bSTMP = Buf("stmp")
    DEC = sb("DEC", [128, 8, 4]); bDEC = Buf("dec")
    SQ = sb("SQ", [128, 1, 512], BF16); bSQ = [Buf("sq0")] * 2
    RS = sb("RS", [128, 2, 512]); bRS = [Buf("rs0"), Buf("rs1")]
    GF = sb("GF", [128, D]); bGF = Buf("gf")
    SHF = sb("SHF", [128, D]); bSHF = Buf("shf")
    CT = sb("CT", [128, 8, 6]); SC = sb("SC", [128, 8, 6], BF16); bSC = Buf("sc")
    NG = sb("NG", [128, 8]); G = sb("G", [128, 8, 6]); SH = sb("SH", [128, 8, 6]); bG = Buf("G")
    RM = sb("RM", [128, 4]); LNVG = sb("LNVG", [128, 4]); LBL = sb("LBL", [128, 2, 4]); GN = sb("GN", [128, 4])
    LBT = sb("LBT", [128, 4]); C0 = sb("C0", [128, 4]); C1 = sb("C1", [128, 4]); NC1 = sb("NC1", [128, 4]); bC = Buf("C")
    IDF = sb("IDF", [128, 128]); IDB = sb("IDB", [128, 128], BF16); bID = Buf("id")
    MKP = sb("MKP", [128, 128], BF16); MKS = sb("MKS", [64, 64], BF16); bMK = Buf("mk")
    WMT = sb("WMT", [128, 4, 128], BF16); bWMT = Buf("wmt")
    WMTS = sb("WMTS", [64, 4, 64], BF16)
    ONESM = sb("ONESM", [128, 128], BF16); ONEC = sb("ONEC", [128, 128], BF16); EPSC = sb("EPSC", [128, 1]); bCONST = Buf("const")
    CG = sb("CG", [128, 4, 128]); CGS = sb("CGS", [128, 4, 64]); bCG = Buf("cg")
    SS = sb("SS", [128, 4]); LNS = sb("LNS", [128, 4]); RSTD = sb("RSTD", [128, 4]); bSS = Buf("ss"); bRSTD = Buf("rstd")
    STATS = sb("STATS", [128, 4, 6]); MV = sb("MV", [128, 4, 2]); bMV = Buf("mv")
    LNV = sb("LNV", [128, 4]); RSV = sb("RSV", [128, 4]); NB = sb("NB", [128, 4]); bRSV = Buf("rsv")
    SS2 = sb("SS2", [128, 4]); LN2 = sb("LN2", [128, 4]); RST2 = sb("RST2", [128, 4]); bSS2 = Buf("ss2"); bRST2 = Buf("rst2")
    bPARAM = Buf("param"); bSCR = Buf("modscr")
    g4 = lambda ap: ap.rearrange("p (g i) -> p g i", g=4)
    MODR = X[0:6, 1, 0:3, :].rearrange("p a b -> p (a b)")
    GFIN6 = X[0:6, 1, 3, :]
    MODF = XN[0:6, 0:2, :].rearrange("p a b -> p (a b)")
    WSPT = g4(XN[:, 2, 0:512]); ROWS = g4(XN[0:2, 2, 512:1024]); LHS = g4(XN[0:2, 3, 0:512])
    WSS = g4(XN[0:64, 3, 512:768]); ROWSS = g4(XN[0:2, 3, 768:1024])
    MKF = X[:, 0, 3, 0:128]; MKSF = X[0:64, 0, 3, 128:192]; CMK = X[:, 0, 3, 256:384]
    bPRO = [bX[0][3]] + bX[1] + bXN
    bMOD = bPRO; bROWS = bPRO

    blk = es.enter_context(nc.Block())
    dbg_bufs = []

    def dump(name, ap, bufs):
        if not dbg:
            return
        dt_ = ap.dtype
        d_ = nc.dram_tensor("dbg_" + name, list(ap.shape), dt_, kind="ExternalOutput").ap()
        b_ = Buf("dbg_" + name); dbg_bufs.append(b_)
        dma(d_, ap, reads=bufs, dbuf=b_)

    def emit_body():
        pl = []

        def pload(dst, src):
            pl.append((dst, src))
        pload(CT[:], cT.rearrange("(k p) b -> p k b", p=128))
        pload(NG[:], ng); pload(RM[:], rmask); pload(LNVG[:], lnvg); pload(LBL[:], lbl); pload(GN[:], gn)
        pload(IDF[:], ident); pload(MKF, maskp); pload(MKSF, masks); pload(CMK, cmask)
        pload(MODR, b_ada.partition_broadcast(6)); pload(MODF, b_adaf.partition_broadcast(6))
        pload(GFIN6, gfin.partition_broadcast(6))
        pload(WSPT, wspT.rearrange("g j i -> j g i"))
        bPARAM.dsem = newsem()
        for dst, src_ in pl:
            prog["sp"].append(lambda h, dst=dst, src_=src_: h.dma_start(out=dst, in_=src_).then_inc(bPARAM.dsem, 16))
            bPARAM.dcnt += 16
        bPARAM.w = (bPARAM.dsem, bPARAM.dcnt)
        for b_ in bPRO:
            b_.w = bPARAM.w
        P_ = [bPARAM] + bPRO

        if stop <= 0:
            return
        op(DVE, lambda: nc.vector.tensor_copy(IDB[:], IDF[:]), reads=P_, writes=[bID])
        op(DVE, lambda: nc.vector.tensor_copy(MKP[:], MKF), reads=P_, writes=[bMK])
        op(DVE, lambda: nc.vector.tensor_copy(MKS[:], MKSF), reads=P_, writes=[bMK])
        op(POOL, lambda: nc.gpsimd.memset(ONESM[:], 1.0 / 128.0), writes=[bCONST])
        op(POOL, lambda: nc.gpsimd.memset(ONEC[:], 1.0), writes=[bCONST])
        op(POOL, lambda: nc.gpsimd.memset(EPSC[:], EPS), writes=[bCONST])
        op(POOL, lambda: nc.gpsimd.memset(BM[:], 1.0), writes=[bCONST])
        op(POOL, lambda: nc.gpsimd.memset(BM[:, 0:512:64], 0.0), writes=[bCONST])
        op(POOL, lambda: nc.gpsimd.memset(BM[:, 512:576:32], 0.0), writes=[bCONST])
        op(POOL, lambda: nc.gpsimd.memset(NHALF[:], -0.5), writes=[bCONST])
        op(POOL, lambda: nc.gpsimd.memset(S["p"][:], 0.0), writes=[bS["p"]])
        op(DVE, lambda: nc.vector.tensor_tensor(LBT[:], LBL[:, 0, :], LBL[:, 1, :], ALU.subtract), reads=P_, writes=[bC])
        op(ACT, lambda: nc.scalar.activation(LBT[:], LBT[:], AF.Tanh, scale=0.5), reads=[bC], writes=[bC])
        op(DVE, lambda: nc.vector.tensor_scalar(C1[:], LBT[:], -0.25, 0.25, ALU.mult, ALU.add), reads=[bC], writes=[bC])
        op(DVE, lambda: nc.vector.tensor_scalar(C0[:], LBT[:], 0.25, 0.75, ALU.mult, ALU.add), reads=[bC], writes=[bC])
        op(DVE, lambda: nc.vector.tensor_scalar(NC1[:], LBT[:], 0.25, -0.25, ALU.mult, ALU.add), reads=[bC], writes=[bC])
        op(DVE, lambda: nc.vector.tensor_tensor(WMT[:], WSPT, CMK.unsqueeze(1).broadcast_to([128, 4, 128]), ALU.mult),
           reads=P_, writes=[bWMT])
        bPD = Buf("prodma")
        op(POOL, lambda: nc.gpsimd.memset(WSS, 0.0), reads=P_, writes=bPRO)
        op(POOL, lambda: nc.gpsimd.memset(ROWSS, 0.0), writes=bPRO)
        op(POOL, lambda: nc.gpsimd.memset(LHS, 1.0), writes=bPRO)
        wsrc = wspT[:, 0:16, 0:16].rearrange("g j i -> j g i")
        dma(WSS[0:16, :, 0:16], wsrc, writes=bPRO, dbuf=bPD)
        dma(WSS[32:48, :, 32:48], wsrc, writes=bPRO, dbuf=bPD)
        op(DVE, lambda: nc.vector.tensor_copy(WMTS[:], WSS), reads=bPRO, writes=[bWMT])
        dma(LHS[0:1, :, :], lnvb_row.rearrange("o (g c) -> o g c", g=4), writes=bPRO, dbuf=bPD)
        dma(ROWS[1:2, :, :], bsp, writes=bPRO, dbuf=bPD)
        dma(ROWSS[1:2, :, 0:16], bsp[:, :, 0:16], writes=bPRO, dbuf=bPD)
        dma(ROWSS[1:2, :, 32:48], bsp[:, :, 0:16], writes=bPRO, dbuf=bPD)
        bk = newbank()
        mm(bk, bk[0][:, 0:512], ONEC[:, :], WMT[:].rearrange("p g i -> p (g i)"), reads=[bCONST, bWMT], last=True)
        op(DVE, lambda: nc.vector.tensor_copy(ROWS[0:1, :, :], bk[0][0:1, 0:512].rearrange("p (g i) -> p g i", g=4)), reads=[bk[1]], writes=bPRO)
        bk2 = newbank()
        mm(bk2, bk2[0][:, 0:256], ONEC[0:64, :], WMTS[:].rearrange("p g i -> p (g i)"), reads=[bCONST, bWMT], last=True)
        op(DVE, lambda: nc.vector.tensor_copy(ROWSS[0:1, :, :], bk2[0][0:1, 0:256].rearrange("p (g i) -> p g i", g=4)), reads=[bk2[1]], writes=bPRO)
        bk3 = newbank()
        for g in range(4):
            mm(bk3, bk3[0][:, g * 128:(g + 1) * 128], LHS[0:2, g, :], ROWS[0:2, g, :], reads=bPRO, last=(g == 3))
        op(DVE, lambda: nc.vector.tensor_copy(CG[:].rearrange("p g i -> p (g i)"), bk3[0][:, 0:512]), reads=[bk3[1]], writes=[bCG])
        bk4 = newbank()
        for g in range(4):
            mm(bk4, bk4[0][:, g * 64:(g + 1) * 64], LHS[0:2, g, :], ROWSS[0:2, g, :], reads=bPRO, last=(g == 3))
        op(DVE, lambda: nc.vector.tensor_copy(CGS[:].rearrange("p g i -> p (g i)"), bk4[0][:, 0:256]), reads=[bk4[1]], writes=[bCG])

        if stop <= 1:
            return
        op(ACT, lambda: nc.scalar.activation(SC[:], CT[:], AF.Silu), reads=P_, writes=[bSC])
        WB = HT
        bWB = [[bHT[0], bHT[1]], [bHT[2], bHT[3]]]
        stg_i = [0]

        stg_slots = [(STG[:, 0, :], bSTG[0]), (STG[:, 1, :], bSTG[1]),
                     (X[:, 0, 0, :], bX[0][0]), (X[:, 0, 1, :], bX[0][1]), (X[:, 0, 2, :], bX[0][2])]
        stg_n = [5]

        def stage_load(src_ap, ncols):
            ap_, b_ = stg_slots[stg_i[0] % stg_n[0]]; stg_i[0] += 1
            dma(ap_[:, 0:ncols], src_ap, writes=[b_], dbuf=b_)
            return ap_, b_
        cast_i = [0]

        def cast_op(dst, src_, reads, writes):
            if cast_i[0] % 2 == 0:
                op(DVE, lambda: nc.vector.tensor_copy(dst, src_), reads=reads, writes=writes)
            else:
                op(ACT, lambda: nc.scalar.activation(dst, src_, AF.Copy), reads=reads, writes=writes)
            cast_i[0] += 1

        def modmat(w_dram, ncolblk, dst):
            bks = [newbank() for _ in range(ncolblk)]
            for k in range(8):
                for pc in range(ncolblk // 2):
                    sa, sb_ = stage_load(w_dram[k * 128:(k + 1) * 128, pc * 1024:(pc + 1) * 1024], 1024)
                    j = (k * (ncolblk // 2) + pc) % 2
                    wbv = WB[:, 2 * j:2 * j + 2, :].rearrange("p a b -> p (a b)")
                    cast_op(wbv, sa[:, 0:1024], [sb_], bWB[j])
                    for hf in range(2):
                        b_ = bks[pc * 2 + hf]
                        mm(b_, b_[0][0:6, 0:512], SC[:, k, :], wbv[:, hf * 512:(hf + 1) * 512], reads=[bSC] + bWB[j],
                           last=(k == 7 or hf == 1), p0=0, npart=6)
            for cbi in range(ncolblk):
                b_ = bks[cbi]
                op(DVE, lambda b_=b_, cbi=cbi: nc.vector.tensor_tensor(dst[:, cbi * 512:(cbi + 1) * 512], b_[0][0:6, 0:512],
                                                                         dst[:, cbi * 512:(cbi + 1) * 512], ALU.add),
                   reads=[b_[1]] + bMOD, writes=bMOD)
        modmat(w_ada, 6, MODR)
        modmat(w_adaf, 4, MODF)
        bk = newbank()
        for c in range(16):
            tr(bk, bk[0][:, c * 6:(c + 1) * 6], MODR[0:6, c * 128:(c + 1) * 128], IDF[0:6, 0:6], reads=bMOD + P_, last=(c == 15))
        op(DVE, lambda: nc.vector.tensor_copy(SH[:].rearrange("p k b -> p (k b)"), bk[0][:, 0:48]), reads=[bk[1]], writes=[bG])
        op(DVE, lambda: nc.vector.scalar_tensor_tensor(G[:], bk[0][:, 48:96].rearrange("p (k b) -> p k b", b=6), 1.0,
                                                        NG[:].unsqueeze(2).broadcast_to([128, 8, 6]), ALU.add, ALU.mult),
           reads=[bk[1]] + P_, writes=[bG])
        op(DVE, lambda: nc.vector.scalar_tensor_tensor(MODF[:, D:2 * D], MODF[:, D:2 * D], 1.0, GFIN6, ALU.add, ALU.mult),
           reads=bMOD + P_, writes=bMOD)
        dma(modscr[:, 0, :], MODR[:, 2 * D:3 * D], reads=bMOD, writes=[bSCR], dbuf=bSCR)
        dma(modscr[:, 1, :], MODF[:, D:2 * D], reads=bMOD, writes=[bSCR], dbuf=bSCR)
        dma(modscr[:, 2, :], MODF[:, 0:D], reads=bMOD, writes=[bSCR], dbuf=bSCR)

        dump("G", G[:], [bG]); dump("SH", SH[:], [bG]); dump("C0", C0[:], [bC]); dump("C1", C1[:], [bC])
        dump("WMT", WMT[:], [bWMT]); dump("WMTS", WMTS[:], [bWMT]); dump("CG", CG[:], [bCG]); dump("CGS", CGS[:], [bCG])
        dump("MODR", MODR, bMOD); dump("MODF", MODF, bMOD); dump("ROWS", ROWS, bPRO); dump("LHS", LHS, bPRO); dump("ONEC", ONEC[:], [bCONST]); dump("EPSC", EPSC[:], [bCONST]); dump("ONESM", ONESM[:], [bCONST])
        if stop <= 2:
            return
        for k in range(8):
            for pc in range(4):
                c0 = pc * 896
                sa, sb_ = stage_load(w_in[k * 128:(k + 1) * 128, c0:c0 + 896], 896)
                cast_op(Win[:, k, c0:c0 + 896], sa[:, 0:896], [sb_], [bWin])
        stg_n[0] = 2

        def restage_pieces(bidx):
            P = []

            def gate():
                dma(SHF[:], modscr[bidx:bidx + 1, 0, :].partition_broadcast(128), reads=[bSCR], writes=[bSHF], dbuf=bSHF)

            def piece(k):
                sa, sb_ = stage_load(w_out[k * 128:(k + 1) * 128, :], 1024)
                if bidx is not None:
                    op(DVE, lambda: nc.vector.tensor_tensor(WOG[:, k, :], sa, SHF[:], ALU.mult),
                       reads=[sb_, bSHF], writes=[bWOG])
                else:
                    op(DVE, lambda: nc.vector.tensor_copy(WOG[:, k, :], sa), reads=[sb_], writes=[bWOG])
            if bidx is not None:
                P.append(gate)
            for k in range(8):
                P.append(lambda k=k: piece(k))
            return P

        def restage_wout(bidx):
            for p_ in restage_pieces(bidx):
                p_()

        class Ctx:
            pass

        def mkctx(xt, T, TS, NT, segs, blocks, bstep, WM, CGm, MKm, yrows, gate_tile=None, sample=False):
            c = Ctx()
            c.xt, c.T, c.TS, c.NT, c.segs, c.blocks, c.bstep = xt, T, TS, NT, segs, blocks, bstep
            c.WM, c.CGm, c.MKm, c.yrows, c.gate_tile, c.sample = WM, CGm, MKm, yrows, gate_tile, sample
            return c

        def fm_chunk(c, col0):
            bk = newbank()
            for k in range(8):
                mm(bk, bk[0][:, 0:c.T], Win[:, k, col0:col0 + 128], HT[:, k, 0:c.T], reads=[bWin, bHT[k]], last=(k == 7))
            return bk

        def tm_chunk(c, t, col0):
            TS = c.TS
            bk = newbank()
            for k in range(8):
                mm(bk, bk[0][:TS, 0:512], HT[:, k, t * TS:(t + 1) * TS], Win[:, k, col0:col0 + 512], reads=[bWin, bHT[k]],
                   last=(k == 7), p0=0, npart=TS)
            return bk

        def g_in_pieces(c):
            T, TS, NT = c.T, c.TS, c.NT
            P = []

            def sq(t):
                xa, xb = c.xt[t]
                op(ACT, lambda: nc.scalar.activation(JUNK[:TS, :], xa, AF.Square, accum_out=SS[:TS, t:t + 1]),
                   reads=[xb], writes=[bSS], multi=True)

            def rstd():
                op(POOL, lambda: nc.gpsimd.tensor_scalar(LNS[:TS, :NT], SS[:TS, :NT], 1.0 / D, EPS, ALU.mult, ALU.add),
                   reads=[bSS], writes=[bRSTD])
                op(POOL, lambda: nc.gpsimd.tensor_tensor(RSTD[:TS, :NT], LNS[:TS, :NT], NHALF[:TS, :NT], ALU.pow), reads=[bRSTD, bCONST], writes=[bRSTD])

            def xn(t):
                xa, xb = c.xt[t]
                if t % 2 == 0:
                    op(DVE, lambda: nc.vector.tensor_scalar(XN[:TS, t, :], xa, RSTD[:TS, t:t + 1], None, ALU.mult),
                       reads=[xb, bRSTD], writes=bXNl[t])
                else:
                    op(ACT, lambda: nc.scalar.activation(XN[:TS, t, :], xa, AF.Copy, scale=RSTD[:TS, t:t + 1]),
                       reads=[xb, bRSTD], writes=bXNl[t])

            def kc(k):
                bk = newbank()
                for t in range(NT):
                    tr(bk, bk[0][:, t * TS:(t + 1) * TS], XN[:TS, t, k * 128:(k + 1) * 128], IDF[:TS, :TS],
                       reads=[bXN[t], bPARAM], last=(t == NT - 1))
                for (c0, c1, b) in c.segs:
                    if k % 2 == 0:
                        op(DVE, lambda: nc.vector.tensor_scalar(
                            HT[:, k, c0:c1], bk[0][:, c0:c1], G[:, k, b:b + 1], SH[:, k, b:b + 1], ALU.mult, ALU.add),
                           reads=[bk[1], bG], writes=[bHT[k]])
                    else:
                        op(ACT, lambda: nc.scalar.activation(
                            HT[:, k, c0:c1], bk[0][:, c0:c1], AF.Identity, bias=SH[:, k, b:b + 1], scale=G[:, k, b:b + 1]),
                           reads=[bk[1], bG], writes=[bHT[k]])
            for t in range(NT):
                P.append(lambda t=t: sq(t))
            P.append(rstd)
            for t in range(NT):
                P.append(lambda t=t: xn(t))
            for k in range(8):
                P.append(lambda k=k: kc(k))
            return P

        def g_in(c):
            for p_ in g_in_pieces(c):
                p_()

        def g_v1(c):
            T, TS, NT = c.T, c.TS, c.NT
            c.vb = []
            for t in range(NT):
                bk = tm_chunk(c, t, OV); c.vb.append(bk); pinned.add(bk[2])
                op(DVE, lambda: nc.vector.bn_stats(STATS[:TS, t, :], bk[0][:TS, 0:512]), reads=[bk[1]], writes=[bMV])
                op(DVE, lambda: nc.vector.bn_aggr(MV[:TS, t, :], STATS[:TS, t, :]), reads=[bMV], writes=[bMV])
            op(POOL, lambda: nc.gpsimd.tensor_scalar(LNV[:TS, :NT], MV[:TS, :NT, 1], 1.0, EPS, ALU.mult, ALU.add), reads=[bMV], writes=[bRSV])
            op(POOL, lambda: nc.gpsimd.tensor_tensor(RSV[:TS, :NT], LNV[:TS, :NT], NHALF[:TS, :NT], ALU.pow), reads=[bRSV, bCONST], writes=[bRSV])
            op(DVE, lambda: nc.vector.scalar_tensor_tensor(NB[:TS, :NT], MV[:TS, :NT, 0], -1.0, RSV[:TS, :NT], ALU.mult, ALU.mult),
               reads=[bMV, bRSV], writes=[bRSV])

        def g_v2(c):
            T, TS, NT, sample = c.T, c.TS, c.NT, c.sample
            for t in range(NT):
                bk = c.vb[t]
                op(ACT, lambda: nc.scalar.activation(VH[:TS, t, :], bk[0][:TS, 0:512], AF.Identity,
                                                     bias=NB[:TS, t:t + 1], scale=RSV[:TS, t:t + 1]),
                   reads=[bk[1], bRSV], writes=[bVH[t]])
                if sample:
                    op(ACT, lambda: nc.scalar.activation(VHF[:TS, :], bk[0][:TS, 0:512], AF.Identity,
                                                         bias=NB[:TS, t:t + 1], scale=RSV[:TS, t:t + 1]),
                       reads=[bk[1], bRSV], writes=[bVHF])
                    op(DVE, lambda: nc.vector.tensor_tensor(VHF[:TS, :], VHF[:TS, :], LGROW[:TS, :], ALU.mult), reads=[bVHF, bLG], writes=[bVHF])
                    op(DVE, lambda: nc.vector.tensor_tensor(VHF[:TS, :], VHF[:TS, :], LBROW[:TS, :], ALU.add), reads=[bVHF, bLG], writes=[bVHF])
                    dma(vso[0:16, :], VHF[0:16, :], reads=[bVHF], dbuf=bVHF)
                    dma(vso[16:32, :], VHF[32:48, :], reads=[bVHF], dbuf=bVHF)
                pinned.discard(bk[2])

        def g_ib1(c, t):
            TS = c.TS
            bk = tm_chunk(c, t, OI)
            op(ACT, lambda: nc.scalar.activation(IB[:TS, t, :], bk[0][:TS, 0:512], AF.Copy), reads=[bk[1]], writes=[bIB[t]])

        def g_gmlp(c):
            T, TS, NT = c.T, c.TS, c.NT
            for cg in range(4):
                j = cg % 2
                bga = fm_chunk(c, OGA + cg * 128)
                op(ACT, lambda: nc.scalar.activation(SG[:, j, 0:T], bga[0][:, 0:T], AF.Silu), reads=[bga[1]], writes=[bSG[j]])
                bu = fm_chunk(c, OU + cg * 128)
                op(DVE, lambda: nc.vector.tensor_tensor(SG[:, j, 0:T], bu[0][:, 0:T], SG[:, j, 0:T], ALU.mult),
                   reads=[bu[1], bSG[j]], writes=[bSG[j]])
                bsp_ = newbank()
                for t in range(NT):
                    mm(bsp_, bsp_[0][:, t * TS:(t + 1) * TS], VH[:TS, t, cg * 128:(cg + 1) * 128], c.WM[:TS, cg, :TS],
                       reads=[bVH[t], bWMT], last=(t == NT - 1))
                op(DVE, lambda: nc.vector.scalar_tensor_tensor(
                    SPb[:, 0:T].rearrange("p (n t) -> p n t", n=NT), bsp_[0][:, 0:T].rearrange("p (n t) -> p n t", n=NT),
                    LNVG[:, cg:cg + 1], c.CGm[:, cg, :TS].unsqueeze(1).broadcast_to([128, NT, TS]), ALU.mult, ALU.add),
                   reads=[bsp_[1], bCG, bPARAM], writes=[bSPb])
                op(POOL, lambda: nc.gpsimd.tensor_tensor(MIX[:, cg, 0:T], SG[:, j, 0:T], SPb[:, 0:T], ALU.mult),
                   reads=[bSG[j], bSPb], writes=[bMIX[cg]])

        KKs = [SG[:, 0, :], SG[:, 1, :]]; bKKs = bSG
        MIXf = MIX[:].rearrange("p k t -> p (k t)").bitcast(F32)
        Q2 = [MIXf[:, h * 512:(h + 1) * 512] for h in range(4)]; bQ2 = [[bMIX[2 * h], bMIX[2 * h + 1]] for h in range(4)]
        EIs = [SPb[:, :], STMP[:].rearrange("p h v -> p (h v)")]; bEIs = [bSPb, bSTMP]

        def g_hA1fq(c):
            T = c.T
            for h in range(4):
                bf_ = fm_chunk(c, OF + h * 128)
                op(ACT, lambda: nc.scalar.activation(TH[h][:, 0:T], bf_[0][:, 0:T], AF.Tanh, scale=0.5), reads=[bf_[1]], writes=[bTH[h]])
                bq_ = fm_chunk(c, OQ + h * 128)
                op(ACT, lambda: nc.scalar.activation(Q[h][:, 0:T], bq_[0][:, 0:T], AF.Silu), reads=[bq_[1]], writes=[bQ[h]])

        def g_hA1g(c):
            T = c.T
            for h in range(4):
                bg_ = fm_chunk(c, OGB + h * 128)
                op(ACT, lambda: nc.scalar.activation(SGB[:, h, 0:T], bg_[0][:, 0:T], AF.Silu), reads=[bg_[1]], writes=[bSGB[h]])

        def g_hA2a(c):
            T = c.T
            for h in range(4):
                op(ACT, lambda: nc.scalar.activation(Q2[h][:, 0:T], TH[h][:, 0:T], AF.Ln, bias=C0[:, h:h + 1], scale=C1[:, h:h + 1]),
                   reads=[bTH[h], bC], writes=bQ2[h])

        def g_hA2b(c):
            T, bstep = c.T, c.bstep
            nblk = len(c.blocks)
            bm0 = 512 if c.sample else 0
            d0 = c.blocks[0][0] + c.blocks[0][1] - 1
            for h in range(4):
                j = h % 2
                op(DVE, lambda: nc.vector.tensor_scalar(KKs[j][:, 0:T], TH[h][:, 0:T], NC1[:, h:h + 1], C1[:, h:h + 1], ALU.mult, ALU.add),
                   reads=[bTH[h], bC], writes=[bKKs[j]])
                op(DVE, lambda: nc.vector.tensor_tensor_scan(E[:, j, 0:T], BM[:, bm0:bm0 + T], Q2[h][:, 0:T], 0.0, ALU.mult, ALU.add),
                   reads=bQ2[h] + [bCONST], writes=[bE[j]])
                op(ACT, lambda: nc.scalar.activation(EIs[j][:, 0:T], E[:, j, 0:T], AF.Exp, scale=-1.0), reads=[bE[j]], writes=[bEIs[j]])
                op(ACT, lambda: nc.scalar.activation(E[:, j, 0:T], E[:, j, 0:T], AF.Exp), reads=[bE[j]], writes=[bE[j]])
                op(ACT, lambda: nc.scalar.activation(DEC[:, 0:nblk, h], E[:, j, d0:T:bstep], AF.Copy), reads=[bE[j]], writes=[bDEC])
                op(DVE, lambda: nc.vector.tensor_tensor(KI[:, h, 0:T], KKs[j][:, 0:T], EIs[j][:, 0:T], ALU.mult), reads=[bKKs[j], bEIs[j]], writes=[bKI[h]])
                op(POOL, lambda: nc.gpsimd.tensor_tensor(QD[:, h, 0:T], Q[h][:, 0:T], E[:, j, 0:T], ALU.mult), reads=[bQ[h], bE[j]], writes=[bQD[h]])
                if h < c.NT:
                    g_ib1(c, h)

        def g_hB(c):
            T, TS, NT = c.T, c.TS, c.NT
            rm0 = 2 if c.sample else 0
            for h in range(4):
                bkt = newbank()
                bktv = bkt[0][:].bitcast(BF16)
                for t in range(NT):
                    tr(bkt, bktv[:TS, t * 128:(t + 1) * 128], KI[:, h, t * TS:(t + 1) * TS], IDB[:, :], reads=[bKI[h], bID], last=(t == NT - 1))
                op(ACT, lambda: nc.scalar.activation(KT[:TS, h, 0, 0:NT * 128], bktv[:TS, 0:NT * 128], AF.Copy, scale=RM[:TS, rm0:rm0 + 1]),
                   reads=[bkt[1], bPARAM], writes=[bKT[h]])
                op(DVE, lambda: nc.vector.tensor_scalar(KT[:TS, h, 1, 0:NT * 128], bktv[:TS, 0:NT * 128], RM[:TS, rm0 + 1:rm0 + 2], None, ALU.mult),
                   reads=[bkt[1], bPARAM], writes=[bKT[h]])
            for h in range(4):
                ba = newbank()
                for t in range(NT):
                    mm(ba, ba[0][:TS, t * TS:(t + 1) * TS], KI[:, h, t * TS:(t + 1) * TS], QD[:, h, t * TS:(t + 1) * TS],
                       reads=[bKI[h], bQD[h]], last=(t == NT - 1), p0=0, npart=TS)
                op(DVE, lambda: nc.vector.tensor_tensor(ATT[:TS, h, 0:T].rearrange("p (n t) -> p n t", n=NT), ba[0][:TS, 0:T].rearrange("p (n t) -> p n t", n=NT),
                                                        c.MKm.unsqueeze(1).broadcast_to([TS, NT, TS]), ALU.mult), reads=[ba[1], bMK], writes=[bATT[h]])

        def g_hC(c, fill=()):
            TS, bstep = c.TS, c.bstep
            fill = list(fill)
            per = -(-len(fill) // max(1, len(c.blocks)))
            for n, (c0, ln, key) in enumerate(c.blocks):
                t, bs = c0 // TS, (c0 % TS) // bstep
                op(ACT, lambda: nc.scalar.activation(SBF[:, n, :, :], S[key], AF.Copy), reads=[bS[key]], writes=[bSBF[n]])
                pbk = newbank()
                for h in range(4):
                    mm(pbk, pbk[0][:, h * 128:(h + 1) * 128], KT[:TS, h, bs, t * 128:(t + 1) * 128],
                       IB[:TS, t, h * 128:(h + 1) * 128], reads=[bKT[h], bIB[t]], last=(h == 3))
                op(DVE, lambda: nc.vector.tensor_tensor(STMP[:], pbk[0][:, 0:512].rearrange("p (h v) -> p h v", h=4), S[key], ALU.add),
                   reads=[pbk[1], bS[key]], writes=[bSTMP])
                op(DVE, lambda: nc.vector.tensor_tensor(S[key], STMP[:], DEC[:, n, :].unsqueeze(2).broadcast_to([128, 4, 128]), ALU.mult),
                   reads=[bSTMP, bDEC], writes=[bS[key]])
                for _ in range(per):
                    if fill:
                        fill.pop(0)()
            while fill:
                fill.pop(0)()

        def g_hD(c):
            T, TS, NT, bstep = c.T, c.TS, c.NT, c.bstep
            for pair in ((0, 1), (2, 3)):
                bos, bms = {}, {}
                for h in pair:
                    bo = newbank(); bos[h] = bo
                    for n, (c0, ln, key) in enumerate(c.blocks):
                        wd = bstep if not c.sample else ln
                        mm(bo, bo[0][:, c0:c0 + wd], SBF[:, n, h, :], QD[:, h, c0:c0 + wd], reads=[bSBF[n], bQD[h]])
                    for t in range(NT):
                        mm(bo, bo[0][:, t * TS:(t + 1) * TS], IB[:TS, t, h * 128:(h + 1) * 128], ATT[:TS, h, t * TS:(t + 1) * TS],
                           reads=[bIB[t], bATT[h]], last=(t == NT - 1))
                for h in pair:
                    j = h % 2; bo = bos[h]
                    op(ACT, lambda: nc.scalar.activation(SQ[:, 0, 0:T], bo[0][:, 0:T], AF.Square), reads=[bo[1]], writes=[bSQ[j]])
                    bm = newbank(); bms[h] = bm
                    mm(bm, bm[0][:, 0:T], ONESM[:, :], SQ[:, 0, 0:T], reads=[bCONST, bSQ[j]], last=True)
                for h in pair:
                    j = h % 2; bm = bms[h]
                    op(ACT, lambda: nc.scalar.activation(RS[:, j, 0:T], bm[0][:, 0:T], AF.Ln, bias=EPSC[:, :]), reads=[bm[1], bCONST], writes=[bRS[j]])
                for h in pair:
                    j = h % 2
                    op(ACT, lambda: nc.scalar.activation(RS[:, j, 0:T], RS[:, j, 0:T], AF.Exp, scale=-0.5), reads=[bRS[j]], writes=[bRS[j]])
                for h in pair:
                    j = h % 2; bo = bos[h]
                    op(DVE, lambda: nc.vector.tensor_tensor(RS[:, j, 0:T], bo[0][:, 0:T], RS[:, j, 0:T], ALU.mult), reads=[bo[1], bRS[j]], writes=[bRS[j]])
                    op(DVE, lambda: nc.vector.scalar_tensor_tensor(MIX[:, 4 + h, 0:T], RS[:, j, 0:T], GN[:, h:h + 1], SGB[:, h, 0:T], ALU.mult, ALU.mult),
                       reads=[bRS[j], bSGB[h], bPARAM], writes=[bMIX[4 + h]])

        def g_out(c):
            T, TS, NT = c.T, c.TS, c.NT
            for t in range(NT):
                xa, xb = c.xt[t]
                for hf in range(2):
                    bk = newbank()
                    for k in range(8):
                        mm(bk, bk[0][:TS, 0:512], MIX[:, k, t * TS:(t + 1) * TS], WOG[:, k, hf * 512:(hf + 1) * 512],
                           reads=[bMIX[k], bWOG], last=(k == 7), p0=0, npart=TS)
                    xh = xa[:, hf * 512:(hf + 1) * 512]
                    if c.gate_tile is None:
                        op(DVE, lambda: nc.vector.tensor_tensor(xh, bk[0][:TS, 0:512], xh, ALU.add), reads=[bk[1], xb], writes=[xb])
                    else:
                        gt, gb_ = c.gate_tile
                        op(DVE, lambda: nc.vector.tensor_tensor(VHF[:TS, :], bk[0][:TS, 0:512], gt[:TS, hf * 512:(hf + 1) * 512], ALU.mult),
                           reads=[bk[1], gb_], writes=[bVHF])
                        op(DVE, lambda: nc.vector.tensor_tensor(xh, VHF[:TS, :], xh, ALU.add), reads=[bVHF, xb], writes=[xb])
                op(ACT, lambda: nc.scalar.activation(JUNK[:TS, :], xa, AF.Square, accum_out=SS2[:TS, t:t + 1]),
                   reads=[xb], writes=[bSS2], multi=True)
            op(POOL, lambda: nc.gpsimd.tensor_scalar(LN2[:TS, :NT], SS2[:TS, :NT], 1.0 / D, EPS, ALU.mult, ALU.add), reads=[bSS2], writes=[bRST2])
            op(POOL, lambda: nc.gpsimd.tensor_tensor(RST2[:TS, :NT], LN2[:TS, :NT], NHALF[:TS, :NT], ALU.pow), reads=[bRST2, bCONST], writes=[bRST2])
            for t in range(NT):
                xa, xb = c.xt[t]
                op(DVE, lambda: nc.vector.scalar_tensor_tensor(xa, xa, RST2[:TS, t:t + 1], GF[:TS, :], ALU.mult, ALU.mult),
                   reads=[xb, bRST2, bGF], writes=[xb])
                op(POOL, lambda: nc.gpsimd.tensor_tensor(xa, xa, SHF[:TS, :], ALU.add), reads=[xb, bSHF], writes=[xb])
                for (dst, r0, r1) in c.yrows[t]:
                    dma(dst, xa[r0:r1, :], reads=[xb], dbuf=xb)

        def load_x(g):
            s = g % 2
            for t in range(4):
                r0 = g * 512 + t * 128
                dma(X[:, s, t, :], xp[r0:r0 + 128, :], writes=[bX[s][t]], dbuf=bX[s][t])

        if stop <= 3:
            return
        NG_ = NSEQ * 4
        blocks_p = [(i * 64, 64, "p") for i in range(8)]
        MKm_p = MKP[:, :]

        def pctx(g):
            s, b = g % 2, g // 4
            xt = [(X[:, s, t, :], bX[s][t]) for t in range(4)]
            yrows = [[(yp[g * 512 + t * 128: g * 512 + (t + 1) * 128, :], 0, 128)] for t in range(4)]
            return mkctx(xt, 512, 128, 4, [(0, 512, b)], blocks_p, 64, WMT, CG, MKm_p, yrows)

        def seq_pieces(b):
            def tail():
                dma(GF[:], modscr[b:b + 1, 1, :].partition_broadcast(128), reads=[bSCR], writes=[bGF], dbuf=bGF)
                dma(SHF[:], modscr[b:b + 1, 2, :].partition_broadcast(128), reads=[bSCR], writes=[bSHF], dbuf=bSHF)
            return restage_pieces(b) + [tail]

        load_x(0)
        cur = pctx(0)
        g_in(cur)
        g_v1(cur)
        g_hA1fq(cur)
        for g in range(NG_):
            b = g // 4
            if g + 1 < NG_:
                load_x(g + 1)
            g_hA1g(cur)
            g_v2(cur)
            g_hA2a(cur)
            g_hA2b(cur)
            g_gmlp(cur)
            g_hB(cur)
            nxt = pctx(g + 1) if g + 1 < NG_ else None
            fl = g_in_pieces(nxt) if nxt is not None else []
            if g % 4 == 0:
                sp_ = seq_pieces(b)
                mix = []
                while fl or sp_:
                    if sp_:
                        mix.append(sp_.pop(0))
                    if fl:
                        mix.append(fl.pop(0))
                    if fl:
                        mix.append(fl.pop(0))
                fl = mix
            g_hC(cur, fill=fl)
            if g % 4 == 3:
                dma(spo[b].rearrange("h d v -> d h v"), S["p"][:, :, :], reads=[bS["p"]], dbuf=bS["p"])
                op(POOL, lambda: nc.gpsimd.memset(S["p"][:], 0.0), writes=[bS["p"]])
            g_hD(cur)
            if nxt is not None:
                g_v1(nxt)
                g_hA1fq(nxt)
            g_out(cur)
            if stop == 10 + g:
                dump('MIX', MIX[:], bMIX); dump('BM', BM[:], [bCONST]); dump('NHALF', NHALF[:], [bCONST]); dump('RSTD', RSTD[:], [bRSTD]); dump('DEC', DEC[:], [bDEC]); dump('RSV', RSV[:], [bRSV]); dump('RST2', RST2[:], [bRST2]); dump('SS', SS[:], [bSS]); dump('SS2', SS2[:], [bSS2]); dump('MV', MV[:], [bMV])
                return
            cur = nxt

        s = NG_ % 2
        xa, xb = X[0:64, s, 0, :], bX[s][0]
        op(POOL, lambda: nc.gpsimd.memset(xa, 0.0), writes=[xb])
        dma(X[0:16, s, 0, :], xs[0:16, :], writes=[xb], dbuf=xb)
        dma(X[32:48, s, 0, :], xs[16:32, :], writes=[xb], dbuf=xb)
        restage_wout(None)
        gt, gtb = X[0:64, s, 1, :], bX[s][1]
        dma(X[0:32, s, 1, :], modscr[4:5, 0, :].partition_broadcast(32), reads=[bSCR], writes=[gtb], dbuf=gtb)
        dma(X[32:64, s, 1, :], modscr[5:6, 0, :].partition_broadcast(32), reads=[bSCR], writes=[gtb], dbuf=gtb)
        LGROW = X[0:64, s, 2, 0:512]; LBROW = X[0:64, s, 2, 512:1024]; bLG = bX[s][2]
        dma(LGROW, lnvg_row.partition_broadcast(64), writes=[bLG], dbuf=bLG)
        dma(LBROW, lnvb_row.partition_broadcast(64), writes=[bLG], dbuf=bLG)
        VHF = X[0:64, s, 3, 0:512]; bVHF = bX[s][3]
        dma(GF[0:32, :], modscr[4:5, 1, :].partition_broadcast(32), reads=[bSCR], writes=[bGF], dbuf=bGF)
        dma(GF[32:64, :], modscr[5:6, 1, :].partition_broadcast(32), reads=[bSCR], writes=[bGF], dbuf=bGF)
        dma(SHF[0:32, :], modscr[4:5, 2, :].partition_broadcast(32), reads=[bSCR], writes=[bSHF], dbuf=bSHF)
        dma(SHF[32:64, :], modscr[5:6, 2, :].partition_broadcast(32), reads=[bSCR], writes=[bSHF], dbuf=bSHF)
        for i, key in enumerate(("s0", "s1")):
            S[key] = X[:, 1 - s, i, 0:512].rearrange("p (h v) -> p h v", h=4)
            bS[key] = bX[1 - s][i]
            dma(S[key], s0[i].rearrange("h d v -> d h v"), writes=[bS[key]], dbuf=bS[key])
        yrows = [[(ys[0:16, :], 0, 16), (ys[16:32, :], 32, 48)]]
        sc = mkctx([(xa, xb)], 64, 64, 1, [(0, 32, 4), (32, 64, 5)], [(0, 16, "s0"), (32, 16, "s1")], 32, WMTS, CGS, MKS[:, :],
                   yrows, gate_tile=(gt, gtb), sample=True)
        g_in(sc); g_v1(sc); g_hA1fq(sc); g_hA1g(sc); g_v2(sc); g_hA2a(sc); g_hA2b(sc); g_gmlp(sc); g_hB(sc); g_hC(sc); g_hD(sc); g_out(sc)
        for i, key in enumerate(("s0", "s1")):
            dma(sso[i].rearrange("h d v -> d h v"), S[key], reads=[bS[key]], dbuf=bS[key])

    def emit_all():
        emit_body()
        allb = [b_ for row in bX for b_ in row] + [bS["p"], bSCR, bGF, bSHF, bPARAM] + bSTG + dbg_bufs
        for b_ in allb:
            if b_.dsem is not None and SP.waited.get(b_.dsem.num, 0) < b_.dcnt:
                prog["sp"].append(lambda h, s_=b_.dsem, v_=b_.dcnt: h.wait_ge(s_, v_))
        for e in (PE, ACT, DVE, POOL):
            if e.cnt:
                prog["sp"].append(lambda h, s_=e.sem, v_=e.cnt: h.wait_ge(s_, v_))

    emit_all()

    @blk.sync
    def _(h):
        for th in prog["sp"]:
            th(h)

    @blk.tensor
    def _(h):
        for th in prog["pe"]:
            th(h)

    @blk.scalar
    def _(h):
        for th in prog["act"]:
            th(h)

    @blk.vector
    def _(h):
        for th in prog["dve"]:
            th(h)

    @blk.gpsimd
    def _(h):
        for th in prog["pool"]:
            th(h)

    es.close()
    return nc


_NC = None


def _consts():
    ident = np.eye(128, dtype=np.float32)
    s = np.arange(128)
    maskp = ((s[:, None] <= s[None, :]) & ((s[:, None] // 64) == (s[None, :] // 64))).astype(np.float32)
    q = np.arange(64)
    valid = (q % 32) < 16
    masks = ((q[:, None] <= q[None, :]) & ((q[:, None] // 32) == (q[None, :] // 32)) & valid[:, None] & valid[None, :]).astype(np.float32)
    cmask = ((s[:, None] // 64) <= (s[None, :] // 64)).astype(np.float32)
    rmask = np.zeros((128, 4), np.float32)
    rmask[:64, 0] = 1; rmask[64:, 1] = 1; rmask[0:16, 2] = 1; rmask[32:48, 3] = 1
    return ident, maskp, masks, cmask, rmask


def kernel(x_prompt, x_sample, c_prompt, c_sample, state_hgrn, norm_g, w_ada, b_ada, w_in, ln_v_g, ln_v_b,
           w_sp, b_sp, lb_logits, gnorm_g, w_out, g_final, w_ada_f, b_ada_f):
    global _NC
    f = lambda a: np.ascontiguousarray(np.asarray(a, dtype=np.float32))
    x_prompt, x_sample, c_prompt, c_sample, state_hgrn = map(f, (x_prompt, x_sample, c_prompt, c_sample, state_hgrn))
    if _NC is None:
        _NC = build()
    nc = _NC
    ident, maskp, masks, cmask, rmask = _consts()
    shared = {
        "ng": f(f(norm_g)[0].reshape(8, 128).T), "w_ada": f(f(w_ada)[0]), "b_ada": f(f(b_ada)[0][None]),
        "w_in": f(f(w_in)[0]), "lnvg": f(f(ln_v_g)[0].reshape(4, 128).T), "lnvg_row": f(f(ln_v_g)[0][None]),
        "lnvb_row": f(f(ln_v_b)[0][None]), "wspT": f(f(w_sp)[0].transpose(0, 2, 1)), "bsp": f(f(b_sp)[0][None]),
        "lbl": f(f(lb_logits).reshape(2, 4, 128).transpose(2, 0, 1)), "gn": f(f(gnorm_g)[0].reshape(4, 128).T),
        "w_out": f(f(w_out)[0]), "gfin": f(f(g_final)[None]), "w_adaf": f(w_ada_f), "b_adaf": f(f(b_ada_f)[None]),
        "ident": ident, "maskp": maskp, "masks": masks, "cmask": cmask, "rmask": rmask,
    }
    in_maps = []
    for i in range(8):
        m = dict(shared)
        m["xp"] = x_prompt[4 * i:4 * i + 4].reshape(NSEQ * L, D)
        m["xs"] = x_sample[2 * i:2 * i + 2].reshape(NSMP * LS, D)
        m["cT"] = f(np.concatenate([c_prompt[4 * i:4 * i + 4], c_sample[2 * i:2 * i + 2]], axis=0).T)
        m["s0"] = f(state_hgrn[0, 2 * i:2 * i + 2])
        in_maps.append(m)
    res = run_bass_kernel_spmd(nc, in_maps, core_ids=list(range(8)))
    r = res.results
    y_prompt = np.concatenate([r[i]["yp"].reshape(4, L, D) for i in range(8)], axis=0).astype(np.float32)
    y_sample = np.concatenate([r[i]["ys"].reshape(2, LS, D) for i in range(8)], axis=0).astype(np.float32)
    sp_ = np.concatenate([r[i]["spo"] for i in range(8)], axis=0)[None].astype(np.float32)
    ss_ = np.concatenate([r[i]["sso"] for i in range(8)], axis=0)[None].astype(np.float32)
    vs_ = np.concatenate([r[i]["vso"].reshape(2, LS, 512) for i in range(8)], axis=0)[None].astype(np.float32)
    return (y_prompt, y_sample, sp_, ss_, vs_)
```

```python
import types
import numpy as np
from contextlib import ExitStack
import concourse.bass as bass
import concourse.mybir as mybir
from concourse.bass_utils import run_bass_kernel_spmd

F32, BF16 = mybir.dt.float32, mybir.dt.bfloat16
AF = mybir.ActivationFunctionType
ALU = mybir.AluOpType

D = 1024
NSEQ = 4
L = 2048
NSMP = 2
LS = 16
EPS = 1e-6
INW = 3584
OU, OV, OGA, OQ, OF, OI, OGB = 0, 512, 1024, 1536, 2048, 2560, 3072


class Buf:
    __slots__ = ("name", "w", "r", "dsem", "dcnt", "started")

    def __init__(self, name):
        self.name = name
        self.w = None
        self.r = {}
        self.dsem = None
        self.dcnt = 0
        self.started = set()


class Eng:
    def __init__(self, name, h, sem):
        self.name, self.h, self.sem = name, h, sem
        self.cnt = 0
        self.waited = {}


def build(stop=99, dbg=None, sub=99):
    nc = bass.Bass("TRN2", target_bir_lowering=False, dynamic_dma_scratch_size=512)
    es = ExitStack()

    def din(name, shape):
        return nc.dram_tensor(name, list(shape), F32, kind="ExternalInput").ap()

    def dout(name, shape):
        return nc.dram_tensor(name, list(shape), F32, kind="ExternalOutput").ap()

    xp = din("xp", [NSEQ * L, D]); xs = din("xs", [NSMP * LS, D]); cT = din("cT", [D, 6])
    s0 = din("s0", [NSMP, 4, 128, 128]); ng = din("ng", [128, 8])
    w_ada = din("w_ada", [D, 3 * D]); b_ada = din("b_ada", [1, 3 * D]); w_in = din("w_in", [D, INW])
    lnvg = din("lnvg", [128, 4]); lnvg_row = din("lnvg_row", [1, 512]); lnvb_row = din("lnvb_row", [1, 512])
    wspT = din("wspT", [4, 128, 128]); bsp = din("bsp", [1, 4, 128]); lbl = din("lbl", [128, 2, 4])
    gn = din("gn", [128, 4]); w_out = din("w_out", [D, D]); gfin = din("gfin", [1, D])
    w_adaf = din("w_adaf", [D, 2 * D]); b_adaf = din("b_adaf", [1, 2 * D])
    ident = din("ident", [128, 128]); maskp = din("maskp", [128, 128]); masks = din("masks", [64, 64])
    cmask = din("cmask", [128, 128]); rmask = din("rmask", [128, 4])
    modscr = nc.dram_tensor("modscr", [6, 3, D], F32, kind="Internal").ap()
    yp = dout("yp", [NSEQ * L, D]); ys = dout("ys", [NSMP * LS, D])
    spo = dout("spo", [NSEQ, 4, 128, 128]); sso = dout("sso", [NSMP, 4, 128, 128]); vso = dout("vso", [NSMP * LS, 512])

    def sb(name, shape, dt=F32):
        return es.enter_context(nc.sbuf_tensor(name, list(shape), dt))

    sempool = [es.enter_context(nc.semaphore("s%d" % i)) for i in range(96)]
    semi = [0]

    def newsem():
        s = sempool[semi[0]]; semi[0] += 1
        return s

    PE = Eng("pe", nc.tensor, newsem()); ACT = Eng("act", nc.scalar, newsem())
    DVE = Eng("dve", nc.vector, newsem()); POOL = Eng("pool", nc.gpsimd, newsem())
    SP = Eng("sp", nc.sync, newsem())

    def _need(reads, writes):
        need = {}

        def add(ev):
            if ev is None:
                return
            s, v = ev
            if need.get(s.num, (s, 0))[1] < v:
                need[s.num] = (s, v)
        for b in reads:
            add(b.w)
        for b in writes:
            add(b.w)
            for ev in b.r.values():
                add(ev)
        return need

    def _waits(eng, reads, writes):
        waits = []
        for num, (s, v) in _need(reads, writes).items():
            if eng.waited.get(num, 0) >= v:
                continue
            if eng is PE and s is PE.sem:
                continue
            if s is PE.sem and eng is not PE:
                assert v <= PE.cnt, "wait on unissued PE increment"
            waits.append((s, v)); eng.waited[num] = v
        return waits

    def _record(ev, reads, writes):
        s, v = ev
        for b in reads:
            old = b.r.get(s.num)
            if old is None or old[1] < v:
                b.r[s.num] = ev
        for b in writes:
            b.w = ev; b.r = {}

    prog = {"pe": [], "act": [], "dve": [], "pool": [], "sp": []}

    def _snap(fn):
        if not fn.__closure__:
            return fn
        cells = []
        for c in fn.__closure__:
            try:
                cells.append(types.CellType(c.cell_contents))
            except ValueError:
                cells.append(c)
        return types.FunctionType(fn.__code__, fn.__globals__, fn.__name__, fn.__defaults__, tuple(cells))

    def op(eng, fn, reads=(), writes=(), inc=True, multi=False):
        fn = _snap(fn)
        waits = _waits(eng, reads, writes)
        emb = None
        if waits and not multi:
            emb = waits[0]; waits = waits[1:]
        if inc:
            eng.cnt += 1
            ev = (eng.sem, eng.cnt)
        else:
            ev = (eng.sem, eng.cnt + 1)
        _record(ev, reads, writes)

        def thunk(h):
            for (s, v) in waits:
                h.wait_ge(s, v)
            ins = fn()
            if emb is not None:
                ins._wait_ge(emb[0], emb[1])
            if inc:
                ins.then_inc(eng.sem, 1)
        prog[eng.name].append(thunk)

    def dma(out_ap, in_ap, reads=(), writes=(), dbuf=None, q=None, **kw):
        q = q or SP
        waits = _waits(q, reads, writes)
        if dbuf.dsem is None:
            dbuf.dsem = newsem()
        dsem = dbuf.dsem
        dbuf.dcnt += 16
        _record((dsem, dbuf.dcnt), reads, writes)

        def thunk(h):
            for (s, v) in waits:
                h.wait_ge(s, v)
            h.dma_start(out=out_ap, in_=in_ap, **kw).then_inc(dsem, 16)
        prog[q.name].append(thunk)

    banks = []
    for i in range(8):
        t = es.enter_context(nc.psum_tensor("bank%d" % i, [128, 512], F32))
        banks.append((t, Buf("bank%d" % i)))
    bi_ = [0]

    pinned = set()

    def newbank():
        while True:
            idx = bi_[0] % 8; bi_[0] += 1
            if idx not in pinned:
                break
        t, b = banks[idx]
        b.started = set()
        return t, b, idx

    def mm(bank, out, lhsT, rhs, reads, last=False, p0=0, npart=128, tp=None):
        bb = bank[1]
        quads = set(range(p0 // 32, (p0 + npart - 1) // 32 + 1))
        if quads.isdisjoint(bb.started):
            start = True; bb.started |= quads
        else:
            assert quads <= bb.started, "psum quadrant start conflict"
            start = False
        kw = {} if tp is None else {"tile_position": tp}
        op(PE, lambda: nc.tensor.matmul(out, lhsT, rhs, start=start, stop=True, skip_group_check=True, **kw),
           reads=reads, writes=[bb], inc=last)

    def tr(bank, out, in_, idn, reads, last=False):
        bb = bank[1]
        op(PE, lambda: nc.tensor.transpose(out, in_, idn), reads=reads, writes=[bb], inc=last)

    Win = sb("Win", [128, 8, INW], BF16); bWin = Buf("Win")
    WOG = sb("WOG", [128, 8, D], BF16); bWOG = Buf("WOG")
    STG = sb("STG", [128, 2, 1024]); bSTG = [Buf("stg%d" % i) for i in range(2)]
    X = sb("X", [128, 2, 4, D]); bX = [[Buf("x%d_%d" % (s, t)) for t in range(4)] for s in range(2)]
    XN = sb("XN", [128, 4, D]); bXN = [Buf("xn%d" % t) for t in range(4)]
    HT = sb("HT", [128, 8, 512], BF16); bHT = [Buf("ht%d" % k) for k in range(8)]
    JUNK = sb("JUNK", [128, D], BF16)
    VH = sb("VH", [128, 4, 512], BF16); bVH = [Buf("vh%d" % t) for t in range(4)]
    SG = sb("SG", [128, 2, 512]); bSG = [Buf("sg0"), Buf("sg1")]
    SPb = sb("SPb", [128, 512]); bSPb = Buf("spb")
    MIX = sb("MIX", [128, 8, 512], BF16); bMIX = [Buf("mix%d" % k) for k in range(8)]
    IB = sb("IB", [128, 4, 512], BF16); bIB = [Buf("ib%d" % t) for t in range(4)]
    TH = [XN[:, h // 2, (h % 2) * 512:(h % 2 + 1) * 512] for h in range(4)]
    Q = [XN[:, 2 + h // 2, (h % 2) * 512:(h % 2 + 1) * 512] for h in range(4)]
    bTH = [Buf("th%d" % h) for h in range(4)]; bQ = [Buf("q%d" % h) for h in range(4)]
    bXNl = [[bXN[0], bTH[0], bTH[1]], [bXN[1], bTH[2], bTH[3]], [bXN[2], bQ[0], bQ[1]], [bXN[3], bQ[2], bQ[3]]]
    BM = sb("BM", [128, 576]); NHALF = sb("NHALF", [128, 4])
    E = sb("E", [128, 2, 512]); bE = [Buf("e0"), Buf("e1")]
    SGB = sb("SGB", [128, 4, 512], BF16); bSGB = [Buf("sgb%d" % i) for i in range(4)]
    KI = sb("KI", [128, 4, 512], BF16); bKI = [Buf("ki%d" % i) for i in range(4)]
    QD = sb("QD", [128, 4, 512], BF16); bQD = [Buf("qd%d" % i) for i in range(4)]
    KT = sb("KT", [128, 4, 2, 512], BF16); bKT = [Buf("kt%d" % i) for i in range(4)]
    ATT = sb("ATT", [128, 4, 512], BF16); bATT = [Buf("att%d" % i) for i in range(4)]
    SBF = sb("SBF", [128, 8, 4, 128], BF16); bSBF = [Buf("sbf%d" % i) for i in range(8)]
    S = {"p": sb("Sp", [128, 4, 128])[:]}
    bS = {"p": Buf("Sp")}
    STMP = sb("STMP", [128, 4, 128]); bSTMP = Buf("stmp")
    DEC = sb("DEC", [128, 8, 4]); bDEC = Buf("dec")
    SQ = sb("SQ", [128, 1, 512], BF16); bSQ = [Buf("sq0")] * 2
    RS = sb("RS", [128, 2, 512]); bRS = [Buf("rs0"), Buf("rs1")]
    GF = sb("GF", [128, D]); bGF = Buf("gf")
    SHF = sb("SHF", [128, D]); bSHF = Buf("shf")
    CT = sb("CT", [128, 8, 6]); SC = sb("SC", [128, 8, 6], BF16); bSC = Buf("sc")
    NG = sb("NG", [128, 8]); G = sb("G", [128, 8, 6]); SH = sb("SH", [128, 8, 6]); bG = Buf("G")
    RM = sb("RM", [128, 4]); LNVG = sb("LNVG", [128, 4]); LBL = sb("LBL", [128, 2, 4]); GN = sb("GN", [128, 4])
    LBT = sb("LBT", [128, 4]); C0 = sb("C0", [128, 4]); C1 = sb("C1", [128, 4]); NC1 = sb("NC1", [128, 4]); bC = Buf("C")
    IDF = sb("IDF", [128, 128]); IDB = sb("IDB", [128, 128], BF16); bID = Buf("id")
    MKP = sb("MKP", [128, 128], BF16); MKS = sb("MKS", [64, 64], BF16); bMK = Buf("mk")
    WMT = sb("WMT", [128, 4, 128], BF16); bWMT = Buf("wmt")
    WMTS = sb("WMTS", [64, 4, 64], BF16)
    ONESM = sb("ONESM", [128, 128], BF16); ONEC = sb("ONEC", [128, 128], BF16); EPSC = sb("EPSC", [128, 1]); bCONST = Buf("const")
    CG = sb("CG", [128, 4, 128]); CGS = sb("CGS", [128, 4, 64]); bCG = Buf("cg")
    SS = sb("SS", [128, 4]); LNS = sb("LNS", [128, 4]); RSTD = sb("RSTD", [128, 4]); bSS = Buf("ss"); bRSTD = Buf("rstd")
    STATS = sb("STATS", [128, 4, 6]); MV = sb("MV", [128, 4, 2]); bMV = Buf("mv")
    LNV = sb("LNV", [128, 4]); RSV = sb("RSV", [128, 4]); NB = sb("NB", [128, 4]); bRSV = Buf("rsv")
    SS2 = sb("SS2", [128, 4]); LN2 = sb("LN2", [128, 4]); RST2 = sb("RST2", [128, 4]); bSS2 = Buf("ss2"); bRST2 = Buf("rst2")
    bPARAM = Buf("param"); bSCR = Buf("modscr")
    g4 = lambda ap: ap.rearrange("p (g i) -> p g i", g=4)
    MODR = X[0:6, 1, 0:3, :].rearrange("p a b -> p (a b)")
    GFIN6 = X[0:6, 1, 3, :]
    MODF = XN[0:6, 0:2, :].rearrange("p a b -> p (a b)")
    WSPT = g4(XN[:, 2, 0:512]); ROWS = g4(XN[0:2, 2, 512:1024]); LHS = g4(XN[0:2, 3, 0:512])
    WSS = g4(XN[0:64, 3, 512:768]); ROWSS = g4(XN[0:2, 3, 768:1024])
    MKF = X[:, 0, 3, 0:128]; MKSF = X[0:64, 0, 3, 128:192]; CMK = X[:, 0, 3, 256:384]
    bPRO = [bX[0][3]] + bX[1] + bXN
    bMOD = bPRO; bROWS = bPRO

    blk = es.enter_context(nc.Block())
    dbg_bufs = []

    def dump(name, ap, bufs):
        if not dbg:
            return
        dt_ = ap.dtype
        d_ = nc.dram_tensor("dbg_" + name, list(ap.shape), dt_, kind="ExternalOutput").ap()
        b_ = Buf("dbg_" + name); dbg_bufs.append(b_)
        dma(d_, ap, reads=bufs, dbuf=b_)

    def emit_body():
        pl = []

        def pload(dst, src):
            pl.append((dst, src))
        pload(CT[:], cT.rearrange("(k p) b -> p k b", p=128))
        pload(NG[:], ng); pload(RM[:], rmask); pload(LNVG[:], lnvg); pload(LBL[:], lbl); pload(GN[:], gn)
        pload(IDF[:], ident); pload(MKF, maskp); pload(MKSF, masks); pload(CMK, cmask)
        pload(MODR, b_ada.partition_broadcast(6)); pload(MODF, b_adaf.partition_broadcast(6))
        pload(GFIN6, gfin.partition_broadcast(6))
        pload(WSPT, wspT.rearrange("g j i -> j g i"))
        bPARAM.dsem = newsem()
        for dst, src_ in pl:
            prog["sp"].append(lambda h, dst=dst, src_=src_: h.dma_start(out=dst, in_=src_).then_inc(bPARAM.dsem, 16))
            bPARAM.dcnt += 16
        bPARAM.w = (bPARAM.dsem, bPARAM.dcnt)
        for b_ in bPRO:
            b_.w = bPARAM.w
        P_ = [bPARAM] + bPRO

        if stop <= 0:
            return
        op(DVE, lambda: nc.vector.tensor_copy(IDB[:], IDF[:]), reads=P_, writes=[bID])
        op(DVE, lambda: nc.vector.tensor_copy(MKP[:], MKF), reads=P_, writes=[bMK])
        op(DVE, lambda: nc.vector.tensor_copy(MKS[:], MKSF), reads=P_, writes=[bMK])
        op(POOL, lambda: nc.gpsimd.memset(ONESM[:], 1.0 / 128.0), writes=[bCONST])
        op(POOL, lambda: nc.gpsimd.memset(ONEC[:], 1.0), writes=[bCONST])
        op(POOL, lambda: nc.gpsimd.memset(EPSC[:], EPS), writes=[bCONST])
        op(POOL, lambda: nc.gpsimd.memset(BM[:], 1.0), writes=[bCONST])
        op(POOL, lambda: nc.gpsimd.memset(BM[:, 0:512:64], 0.0), writes=[bCONST])
        op(POOL, lambda: nc.gpsimd.memset(BM[:, 512:576:32], 0.0), writes=[bCONST])
        op(POOL, lambda: nc.gpsimd.memset(NHALF[:], -0.5), writes=[bCONST])
        op(POOL, lambda: nc.gpsimd.memset(S["p"][:], 0.0), writes=[bS["p"]])
        op(DVE, lambda: nc.vector.tensor_tensor(LBT[:], LBL[:, 0, :], LBL[:, 1, :], ALU.subtract), reads=P_, writes=[bC])
        op(ACT, lambda: nc.scalar.activation(LBT[:], LBT[:], AF.Tanh, scale=0.5), reads=[bC], writes=[bC])
        op(DVE, lambda: nc.vector.tensor_scalar(C1[:], LBT[:], -0.25, 0.25, ALU.mult, ALU.add), reads=[bC], writes=[bC])
        op(DVE, lambda: nc.vector.tensor_scalar(C0[:], LBT[:], 0.25, 0.75, ALU.mult, ALU.add), reads=[bC], writes=[bC])
        op(DVE, lambda: nc.vector.tensor_scalar(NC1[:], LBT[:], 0.25, -0.25, ALU.mult, ALU.add), reads=[bC], writes=[bC])
        op(DVE, lambda: nc.vector.tensor_tensor(WMT[:], WSPT, CMK.unsqueeze(1).broadcast_to([128, 4, 128]), ALU.mult),
           reads=P_, writes=[bWMT])
        bPD = Buf("prodma")
        op(POOL, lambda: nc.gpsimd.memset(WSS, 0.0), reads=P_, writes=bPRO)
        op(POOL, lambda: nc.gpsimd.memset(ROWSS, 0.0), writes=bPRO)
        op(POOL, lambda: nc.gpsimd.memset(LHS, 1.0), writes=bPRO)
        wsrc = wspT[:, 0:16, 0:16].rearrange("g j i -> j g i")
        dma(WSS[0:16, :, 0:16], wsrc, writes=bPRO, dbuf=bPD)
        dma(WSS[32:48, :, 32:48], wsrc, writes=bPRO, dbuf=bPD)
        op(DVE, lambda: nc.vector.tensor_copy(WMTS[:], WSS), reads=bPRO, writes=[bWMT])
        dma(LHS[0:1, :, :], lnvb_row.rearrange("o (g c) -> o g c", g=4), writes=bPRO, dbuf=bPD)
        dma(ROWS[1:2, :, :], bsp, writes=bPRO, dbuf=bPD)
        dma(ROWSS[1:2, :, 0:16], bsp[:, :, 0:16], writes=bPRO, dbuf=bPD)
        dma(ROWSS[1:2, :, 32:48], bsp[:, :, 0:16], writes=bPRO, dbuf=bPD)
        bk = newbank()
        mm(bk, bk[0][:, 0:512], ONEC[:, :], WMT[:].rearrange("p g i -> p (g i)"), reads=[bCONST, bWMT], last=True)
        op(DVE, lambda: nc.vector.tensor_copy(ROWS[0:1, :, :], bk[0][0:1, 0:512].rearrange("p (g i) -> p g i", g=4)), reads=[bk[1]], writes=bPRO)
        bk2 = newbank()
        mm(bk2, bk2[0][:, 0:256], ONEC[0:64, :], WMTS[:].rearrange("p g i -> p (g i)"), reads=[bCONST, bWMT], last=True)
        op(DVE, lambda: nc.vector.tensor_copy(ROWSS[0:1, :, :], bk2[0][0:1, 0:256].rearrange("p (g i) -> p g i", g=4)), reads=[bk2[1]], writes=bPRO)
        bk3 = newbank()
        for g in range(4):
            mm(bk3, bk3[0][:, g * 128:(g + 1) * 128], LHS[0:2, g, :], ROWS[0:2, g, :], reads=bPRO, last=(g == 3))
        op(DVE, lambda: nc.vector.tensor_copy(CG[:].rearrange("p g i -> p (g i)"), bk3[0][:, 0:512]), reads=[bk3[1]], writes=[bCG])
        bk4 = newbank()
        for g in range(4):
            mm(bk4, bk4[0][:, g * 64:(g + 1) * 64], LHS[0:2, g, :], ROWSS[0:2, g, :], reads=bPRO, last=(g == 3))
        op(DVE, lambda: nc.vector.tensor_copy(CGS[:].rearrange("p g i -> p (g i)"), bk4[0][:, 0:256]), reads=[bk4[1]], writes=[bCG])

        if stop <= 1:
            return
        op(ACT, lambda: nc.scalar.activation(SC[:], CT[:], AF.Silu), reads=P_, writes=[bSC])
        WB = HT
        bWB = [[bHT[0], bHT[1]], [bHT[2], bHT[3]]]
        stg_i = [0]

        stg_slots = [(STG[:, 0, :], bSTG[0]), (STG[:, 1, :], bSTG[1]),
                     (X[:, 0, 0, :], bX[0][0]), (X[:, 0, 1, :], bX[0][1]), (X[:, 0, 2, :], bX[0][2])]
        stg_n = [5]

        def stage_load(src_ap, ncols):
            ap_, b_ = stg_slots[stg_i[0] % stg_n[0]]; stg_i[0] += 1
            dma(ap_[:, 0:ncols], src_ap, writes=[b_], dbuf=b_)
            return ap_, b_
        cast_i = [0]

        def cast_op(dst, src_, reads, writes):
            if cast_i[0] % 2 == 0:
                op(DVE, lambda: nc.vector.tensor_copy(dst, src_), reads=reads, writes=writes)
            else:
                op(ACT, lambda: nc.scalar.activation(dst, src_, AF.Copy), reads=reads, writes=writes)
            cast_i[0] += 1

        def modmat(w_dram, ncolblk, dst):
            bks = [newbank() for _ in range(ncolblk)]
            for k in range(8):
                for pc in range(ncolblk // 2):
                    sa, sb_ = stage_load(w_dram[k * 128:(k + 1) * 128, pc * 1024:(pc + 1) * 1024], 1024)
                    j = (k * (ncolblk // 2) + pc) % 2
                    wbv = WB[:, 2 * j:2 * j + 2, :].rearrange("p a b -> p (a b)")
                    cast_op(wbv, sa[:, 0:1024], [sb_], bWB[j])
                    for hf in range(2):
                        b_ = bks[pc * 2 + hf]
                        mm(b_, b_[0][0:6, 0:512], SC[:, k, :], wbv[:, hf * 512:(hf + 1) * 512], reads=[bSC] + bWB[j],
                           last=(k == 7 or hf == 1), p0=0, npart=6)
            for cbi in range(ncolblk):
                b_ = bks[cbi]
                op(DVE, lambda b_=b_, cbi=cbi: nc.vector.tensor_tensor(dst[:, cbi * 512:(cbi + 1) * 512], b_[0][0:6, 0:512],
                                                                         dst[:, cbi * 512:(cbi + 1) * 512], ALU.add),
                   reads=[b_[1]] + bMOD, writes=bMOD)
        modmat(w_ada, 6, MODR)
        modmat(w_adaf, 4, MODF)
        bk = newbank()
        for c in range(16):
            tr(bk, bk[0][:, c * 6:(c + 1) * 6], MODR[0:6, c * 128:(c + 1) * 128], IDF[0:6, 0:6], reads=bMOD + P_, last=(c == 15))
        op(DVE, lambda: nc.vector.tensor_copy(SH[:].rearrange("p k b -> p (k b)"), bk[0][:, 0:48]), reads=[bk[1]], writes=[bG])
        op(DVE, lambda: nc.vector.scalar_tensor_tensor(G[:], bk[0][:, 48:96].rearrange("p (k b) -> p k b", b=6), 1.0,
                                                        NG[:].unsqueeze(2).broadcast_to([128, 8, 6]), ALU.add, ALU.mult),
           reads=[bk[1]] + P_, writes=[bG])
        op(DVE, lambda: nc.vector.scalar_tensor_tensor(MODF[:, D:2 * D], MODF[:, D:2 * D], 1.0, GFIN6, ALU.add, ALU.mult),
           reads=bMOD + P_, writes=bMOD)
        dma(modscr[:, 0, :], MODR[:, 2 * D:3 * D], reads=bMOD, writes=[bSCR], dbuf=bSCR)
        dma(modscr[:, 1, :], MODF[:, D:2 * D], reads=bMOD, writes=[bSCR], dbuf=bSCR)
        dma(modscr[:, 2, :], MODF[:, 0:D], reads=bMOD, writes=[bSCR], dbuf=bSCR)

        dump("G", G[:], [bG]); dump("SH", SH[:], [bG]); dump("C0", C0[:], [bC]); dump("C1", C1[:], [bC])
        dump("WMT", WMT[:], [bWMT]); dump("WMTS", WMTS[:], [bWMT]); dump("CG", CG[:], [bCG]); dump("CGS", CGS[:], [bCG])
        dump("MODR", MODR, bMOD); dump("MODF", MODF, bMOD); dump("ROWS", ROWS, bPRO); dump("LHS", LHS, bPRO); dump("ONEC", ONEC[:], [bCONST]); dump("EPSC", EPSC[:], [bCONST]); dump("ONESM", ONESM[:], [bCONST])
        if stop <= 2:
            return
        for k in range(8):
            for pc in range(4):
                c0 = pc * 896
                sa, sb_ = stage_load(w_in[k * 128:(k + 1) * 128, c0:c0 + 896], 896)
                cast_op(Win[:, k, c0:c0 + 896], sa[:, 0:896], [sb_], [bWin])
        stg_n[0] = 2

        def restage_pieces(bidx):
            P = []

            def gate():
                dma(SHF[:], modscr[bidx:bidx + 1, 0, :].partition_broadcast(128), reads=[bSCR], writes=[bSHF], dbuf=bSHF)

            def piece(k):
                sa, sb_ = stage_load(w_out[k * 128:(k + 1) * 128, :], 1024)
                if bidx is not None:
                    op(DVE, lambda: nc.vector.tensor_tensor(WOG[:, k, :], sa, SHF[:], ALU.mult),
                       reads=[sb_, bSHF], writes=[bWOG])
                else:
                    op(DVE, lambda: nc.vector.tensor_copy(WOG[:, k, :], sa), reads=[sb_], writes=[bWOG])
            if bidx is not None:
                P.append(gate)
            for k in range(8):
                P.append(lambda k=k: piece(k))
            return P

        def restage_wout(bidx):
            for p_ in restage_pieces(bidx):
                p_()

        class Ctx:
            pass

        def mkctx(xt, T, TS, NT, segs, blocks, bstep, WM, CGm, MKm, yrows, gate_tile=None, sample=False):
            c = Ctx()
            c.xt, c.T, c.TS, c.NT, c.segs, c.blocks, c.bstep = xt, T, TS, NT, segs, blocks, bstep
            c.WM, c.CGm, c.MKm, c.yrows, c.gate_tile, c.sample = WM, CGm, MKm, yrows, gate_tile, sample
            return c

        def fm_chunk(c, col0):
            bk = newbank()
            for k in range(8):
                mm(bk, bk[0][:, 0:c.T], Win[:, k, col0:col0 + 128], HT[:, k, 0:c.T], reads=[bWin, bHT[k]], last=(k == 7))
            return bk

        def tm_chunk(c, t, col0):
            TS = c.TS
            bk = newbank()
            for k in range(8):
                mm(bk, bk[0][:TS, 0:512], HT[:, k, t * TS:(t + 1) * TS], Win[:, k, col0:col0 + 512], reads=[bWin, bHT[k]],
                   last=(k == 7), p0=0, npart=TS)
            return bk

        def g_in_pieces(c):
            T, TS, NT = c.T, c.TS, c.NT
            P = []

            def sq(t):
                xa, xb = c.xt[t]
                op(ACT, lambda: nc.scalar.activation(JUNK[:TS, :], xa, AF.Square, accum_out=SS[:TS, t:t + 1]),
                   reads=[xb], writes=[bSS], multi=True)

            def rstd():
                op(POOL, lambda: nc.gpsimd.tensor_scalar(LNS[:TS, :NT], SS[:TS, :NT], 1.0 / D, EPS, ALU.mult, ALU.add),
                   reads=[bSS], writes=[bRSTD])
                op(POOL, lambda: nc.gpsimd.tensor_tensor(RSTD[:TS, :NT], LNS[:TS, :NT], NHALF[:TS, :NT], ALU.pow), reads=[bRSTD, bCONST], writes=[bRSTD])

            def xn(t):
                xa, xb = c.xt[t]
                if t % 2 == 0:
                    op(DVE, lambda: nc.vector.tensor_scalar(XN[:TS, t, :], xa, RSTD[:TS, t:t + 1], None, ALU.mult),
                       reads=[xb, bRSTD], writes=bXNl[t])
                else:
                    op(ACT, lambda: nc.scalar.activation(XN[:TS, t, :], xa, AF.Copy, scale=RSTD[:TS, t:t + 1]),
                       reads=[xb, bRSTD], writes=bXNl[t])

            def kc(k):
                bk = newbank()
                for t in range(NT):
                    tr(bk, bk[0][:, t * TS:(t + 1) * TS], XN[:TS, t, k * 128:(k + 1) * 128], IDF[:TS, :TS],
                       reads=[bXN[t], bPARAM], last=(t == NT - 1))
                for (c0, c1, b) in c.segs:
                    if k % 2 == 0:
                        op(DVE, lambda: nc.vector.tensor_scalar(
                            HT[:, k, c0:c1], bk[0][:, c0:c1], G[:, k, b:b + 1], SH[:, k, b:b + 1], ALU.mult, ALU.add),
                           reads=[bk[1], bG], writes=[bHT[k]])
                    else:
                        op(ACT, lambda: nc.scalar.activation(
                            HT[:, k, c0:c1], bk[0][:, c0:c1], AF.Identity, bias=SH[:, k, b:b + 1], scale=G[:, k, b:b + 1]),
                           reads=[bk[1], bG], writes=[bHT[k]])
            for t in range(NT):
                P.append(lambda t=t: sq(t))
            P.append(rstd)
            for t in range(NT):
                P.append(lambda t=t: xn(t))
            for k in range(8):
                P.append(lambda k=k: kc(k))
            return P

        def g_in(c):
            for p_ in g_in_pieces(c):
                p_()

        def g_v1(c):
            T, TS, NT = c.T, c.TS, c.NT
            c.vb = []
            for t in range(NT):
                bk = tm_chunk(c, t, OV); c.vb.append(bk); pinned.add(bk[2])
                op(DVE, lambda: nc.vector.bn_stats(STATS[:TS, t, :], bk[0][:TS, 0:512]), reads=[bk[1]], writes=[bMV])
                op(DVE, lambda: nc.vector.bn_aggr(MV[:TS, t, :], STATS[:TS, t, :]), reads=[bMV], writes=[bMV])
            op(POOL, lambda: nc.gpsimd.tensor_scalar(LNV[:TS, :NT], MV[:TS, :NT, 1], 1.0, EPS, ALU.mult, ALU.add), reads=[bMV], writes=[bRSV])
            op(POOL, lambda: nc.gpsimd.tensor_tensor(RSV[:TS, :NT], LNV[:TS, :NT], NHALF[:TS, :NT], ALU.pow), reads=[bRSV, bCONST], writes=[bRSV])
            op(DVE, lambda: nc.vector.scalar_tensor_tensor(NB[:TS, :NT], MV[:TS, :NT, 0], -1.0, RSV[:TS, :NT], ALU.mult, ALU.mult),
               reads=[bMV, bRSV], writes=[bRSV])

        def g_v2(c):
            T, TS, NT, sample = c.T, c.TS, c.NT, c.sample
            for t in range(NT):
                bk = c.vb[t]
                op(ACT, lambda: nc.scalar.activation(VH[:TS, t, :], bk[0][:TS, 0:512], AF.Identity,
                                                     bias=NB[:TS, t:t + 1], scale=RSV[:TS, t:t + 1]),
                   reads=[bk[1], bRSV], writes=[bVH[t]])
                if sample:
                    op(ACT, lambda: nc.scalar.activation(VHF[:TS, :], bk[0][:TS, 0:512], AF.Identity,
                                                         bias=NB[:TS, t:t + 1], scale=RSV[:TS, t:t + 1]),
                       reads=[bk[1], bRSV], writes=[bVHF])
                    op(DVE, lambda: nc.vector.tensor_tensor(VHF[:TS, :], VHF[:TS, :], LGROW[:TS, :], ALU.mult), reads=[bVHF, bLG], writes=[bVHF])
                    op(DVE, lambda: nc.vector.tensor_tensor(VHF[:TS, :], VHF[:TS, :], LBROW[:TS, :], ALU.add), reads=[bVHF, bLG], writes=[bVHF])
                    dma(vso[0:16, :], VHF[0:16, :], reads=[bVHF], dbuf=bVHF)
                    dma(vso[16:32, :], VHF[32:48, :], reads=[bVHF], dbuf=bVHF)
                pinned.discard(bk[2])

        def g_ib1(c, t):
            TS = c.TS
            bk = tm_chunk(c, t, OI)
            op(ACT, lambda: nc.scalar.activation(IB[:TS, t, :], bk[0][:TS, 0:512], AF.Copy), reads=[bk[1]], writes=[bIB[t]])

        def g_gmlp(c):
            T, TS, NT = c.T, c.TS, c.NT
            for cg in range(4):
                j = cg % 2
                bga = fm_chunk(c, OGA + cg * 128)
                op(ACT, lambda: nc.scalar.activation(SG[:, j, 0:T], bga[0][:, 0:T], AF.Silu), reads=[bga[1]], writes=[bSG[j]])
                bu = fm_chunk(c, OU + cg * 128)
                op(DVE, lambda: nc.vector.tensor_tensor(SG[:, j, 0:T], bu[0][:, 0:T], SG[:, j, 0:T], ALU.mult),
                   reads=[bu[1], bSG[j]], writes=[bSG[j]])
                bsp_ = newbank()
                for t in range(NT):
                    mm(bsp_, bsp_[0][:, t * TS:(t + 1) * TS], VH[:TS, t, cg * 128:(cg + 1) * 128], c.WM[:TS, cg, :TS],
                       reads=[bVH[t], bWMT], last=(t == NT - 1))
                spv, spb_ = EIs[cg % 2], bEIs[cg % 2]
                op(DVE, lambda: nc.vector.scalar_tensor_tensor(
                    spv[:, 0:T].rearrange("p (n t) -> p n t", n=NT), bsp_[0][:, 0:T].rearrange("p (n t) -> p n t", n=NT),
                    LNVG[:, cg:cg + 1], c.CGm[:, cg, :TS].unsqueeze(1).broadcast_to([128, NT, TS]), ALU.mult, ALU.add),
                   reads=[bsp_[1], bCG, bPARAM], writes=[spb_])
                op(POOL, lambda: nc.gpsimd.tensor_tensor(MIX[:, cg, 0:T], SG[:, j, 0:T], spv[:, 0:T], ALU.mult),
                   reads=[bSG[j], spb_], writes=[bMIX[cg]])

        KKs = [SG[:, 0, :], SG[:, 1, :]]; bKKs = bSG
        MIXf = MIX[:].rearrange("p k t -> p (k t)").bitcast(F32)
        Q2 = [MIXf[:, h * 512:(h + 1) * 512] for h in range(4)]; bQ2 = [[bMIX[2 * h], bMIX[2 * h + 1]] for h in range(4)]
        EIs = [SPb[:, :], STMP[:].rearrange("p h v -> p (h v)")]; bEIs = [bSPb, bSTMP]

        def g_hA1(c):
            T = c.T
            for h in range(4):
                bf_ = fm_chunk(c, OF + h * 128)
                op(ACT, lambda: nc.scalar.activation(TH[h][:, 0:T], bf_[0][:, 0:T], AF.Tanh, scale=0.5), reads=[bf_[1]], writes=[bTH[h]])
                bq_ = fm_chunk(c, OQ + h * 128)
                op(ACT, lambda: nc.scalar.activation(Q[h][:, 0:T], bq_[0][:, 0:T], AF.Silu), reads=[bq_[1]], writes=[bQ[h]])
                bg_ = fm_chunk(c, OGB + h * 128)
                op(ACT, lambda: nc.scalar.activation(SGB[:, h, 0:T], bg_[0][:, 0:T], AF.Silu), reads=[bg_[1]], writes=[bSGB[h]])

        def g_hA2a(c):
            T = c.T
            for h in range(4):
                op(ACT, lambda: nc.scalar.activation(Q2[h][:, 0:T], TH[h][:, 0:T], AF.Ln, bias=C0[:, h:h + 1], scale=C1[:, h:h + 1]),
                   reads=[bTH[h], bC], writes=bQ2[h])

        def g_hA2b(c):
            T, bstep = c.T, c.bstep
            nblk = len(c.blocks)
            bm0 = 512 if c.sample else 0
            d0 = c.blocks[0][0] + c.blocks[0][1] - 1
            for h in range(4):
                j = h % 2
                op(DVE, lambda: nc.vector.tensor_scalar(KKs[j][:, 0:T], TH[h][:, 0:T], NC1[:, h:h + 1], C1[:, h:h + 1], ALU.mult, ALU.add),
                   reads=[bTH[h], bC], writes=[bKKs[j]])
                op(DVE, lambda: nc.vector.tensor_tensor_scan(E[:, j, 0:T], BM[:, bm0:bm0 + T], Q2[h][:, 0:T], 0.0, ALU.mult, ALU.add),
                   reads=bQ2[h] + [bCONST], writes=[bE[j]])
                op(ACT, lambda: nc.scalar.activation(EIs[j][:, 0:T], E[:, j, 0:T], AF.Exp, scale=-1.0), reads=[bE[j]], writes=[bEIs[j]])
                op(ACT, lambda: nc.scalar.activation(E[:, j, 0:T], E[:, j, 0:T], AF.Exp), reads=[bE[j]], writes=[bE[j]])
                op(ACT, lambda: nc.scalar.activation(DEC[:, 0:nblk, h], E[:, j, d0:T:bstep], AF.Copy), reads=[bE[j]], writes=[bDEC])
                op(DVE, lambda: nc.vector.tensor_tensor(KI[:, h, 0:T], KKs[j][:, 0:T], EIs[j][:, 0:T], ALU.mult), reads=[bKKs[j], bEIs[j]], writes=[bKI[h]])
                op(POOL, lambda: nc.gpsimd.tensor_tensor(QD[:, h, 0:T], Q[h][:, 0:T], E[:, j, 0:T], ALU.mult), reads=[bQ[h], bE[j]], writes=[bQD[h]])
                if h < c.NT:
                    g_ib1(c, h)

        def g_hB(c):
            T, TS, NT = c.T, c.TS, c.NT
            rm0 = 2 if c.sample else 0
            for h in range(4):
                bkt = newbank()
                bktv = bkt[0][:].bitcast(BF16)
                for t in range(NT):
                    tr(bkt, bktv[:TS, t * 128:(t + 1) * 128], KI[:, h, t * TS:(t + 1) * TS], IDB[:, :], reads=[bKI[h], bID], last=(t == NT - 1))
                op(ACT, lambda: nc.scalar.activation(KT[:TS, h, 0, 0:NT * 128], bktv[:TS, 0:NT * 128], AF.Copy, scale=RM[:TS, rm0:rm0 + 1]),
                   reads=[bkt[1], bPARAM], writes=[bKT[h]])
                op(DVE, lambda: nc.vector.tensor_scalar(KT[:TS, h, 1, 0:NT * 128], bktv[:TS, 0:NT * 128], RM[:TS, rm0 + 1:rm0 + 2], None, ALU.mult),
                   reads=[bkt[1], bPARAM], writes=[bKT[h]])
            for h in range(4):
                ba = newbank()
                for t in range(NT):
                    mm(ba, ba[0][:TS, t * TS:(t + 1) * TS], KI[:, h, t * TS:(t + 1) * TS], QD[:, h, t * TS:(t + 1) * TS],
                       reads=[bKI[h], bQD[h]], last=(t == NT - 1), p0=0, npart=TS)
                op(DVE, lambda: nc.vector.tensor_tensor(ATT[:TS, h, 0:T].rearrange("p (n t) -> p n t", n=NT), ba[0][:TS, 0:T].rearrange("p (n t) -> p n t", n=NT),
                                                        c.MKm.unsqueeze(1).broadcast_to([TS, NT, TS]), ALU.mult), reads=[ba[1], bMK], writes=[bATT[h]])

        def g_hC(c, fill=()):
            TS, bstep = c.TS, c.bstep
            fill = list(fill)
            per = -(-len(fill) // max(1, len(c.blocks)))
            for n, (c0, ln, key) in enumerate(c.blocks):
                t, bs = c0 // TS, (c0 % TS) // bstep
                op(ACT, lambda: nc.scalar.activation(SBF[:, n, :, :], S[key], AF.Copy), reads=[bS[key]], writes=[bSBF[n]])
                pbk = newbank()
                for h in range(4):
                    mm(pbk, pbk[0][:, h * 128:(h + 1) * 128], KT[:TS, h, bs, t * 128:(t + 1) * 128],
                       IB[:TS, t, h * 128:(h + 1) * 128], reads=[bKT[h], bIB[t]], last=(h == 3))
                op(DVE, lambda: nc.vector.tensor_tensor(STMP[:], pbk[0][:, 0:512].rearrange("p (h v) -> p h v", h=4), S[key], ALU.add),
                   reads=[pbk[1], bS[key]], writes=[bSTMP])
                op(DVE, lambda: nc.vector.tensor_tensor(S[key], STMP[:], DEC[:, n, :].unsqueeze(2).broadcast_to([128, 4, 128]), ALU.mult),
                   reads=[bSTMP, bDEC], writes=[bS[key]])
                for _ in range(per):
                    if fill:
                        fill.pop(0)()
            while fill:
                fill.pop(0)()

        def g_hD(c):
            T, TS, NT, bstep = c.T, c.TS, c.NT, c.bstep
            for pair in ((0, 1), (2, 3)):
                bos, bms = {}, {}
                for h in pair:
                    bo = newbank(); bos[h] = bo
                    for n, (c0, ln, key) in enumerate(c.blocks):
                        wd = bstep if not c.sample else ln
                        mm(bo, bo[0][:, c0:c0 + wd], SBF[:, n, h, :], QD[:, h, c0:c0 + wd], reads=[bSBF[n], bQD[h]])
                    for t in range(NT):
                        mm(bo, bo[0][:, t * TS:(t + 1) * TS], IB[:TS, t, h * 128:(h + 1) * 128], ATT[:TS, h, t * TS:(t + 1) * TS],
                           reads=[bIB[t], bATT[h]], last=(t == NT - 1))
                for h in pair:
                    j = h % 2; bo = bos[h]
                    op(ACT, lambda: nc.scalar.activation(SQ[:, 0, 0:T], bo[0][:, 0:T], AF.Square), reads=[bo[1]], writes=[bSQ[j]])
                    bm = newbank(); bms[h] = bm
                    mm(bm, bm[0][:, 0:T], ONESM[:, :], SQ[:, 0, 0:T], reads=[bCONST, bSQ[j]], last=True)
                for h in pair:
                    j = h % 2; bm = bms[h]
                    op(ACT, lambda: nc.scalar.activation(RS[:, j, 0:T], bm[0][:, 0:T], AF.Ln, bias=EPSC[:, :]), reads=[bm[1], bCONST], writes=[bRS[j]])
                for h in pair:
                    j = h % 2
                    op(ACT, lambda: nc.scalar.activation(RS[:, j, 0:T], RS[:, j, 0:T], AF.Exp, scale=-0.5), reads=[bRS[j]], writes=[bRS[j]])
                for h in pair:
                    j = h % 2; bo = bos[h]
                    op(DVE, lambda: nc.vector.tensor_tensor(RS[:, j, 0:T], bo[0][:, 0:T], RS[:, j, 0:T], ALU.mult), reads=[bo[1], bRS[j]], writes=[bRS[j]])
                    op(DVE, lambda: nc.vector.scalar_tensor_tensor(MIX[:, 4 + h, 0:T], RS[:, j, 0:T], GN[:, h:h + 1], SGB[:, h, 0:T], ALU.mult, ALU.mult),
                       reads=[bRS[j], bSGB[h], bPARAM], writes=[bMIX[4 + h]])

        def g_out(c):
            T, TS, NT = c.T, c.TS, c.NT
            for t in range(NT):
                xa, xb = c.xt[t]
                for hf in range(2):
                    bk = newbank()
                    for k in range(8):
                        mm(bk, bk[0][:TS, 0:512], MIX[:, k, t * TS:(t + 1) * TS], WOG[:, k, hf * 512:(hf + 1) * 512],
                           reads=[bMIX[k], bWOG], last=(k == 7), p0=0, npart=TS)
                    xh = xa[:, hf * 512:(hf + 1) * 512]
                    if c.gate_tile is None:
                        op(DVE, lambda: nc.vector.tensor_tensor(xh, bk[0][:TS, 0:512], xh, ALU.add), reads=[bk[1], xb], writes=[xb])
                    else:
                        gt, gb_ = c.gate_tile
                        op(DVE, lambda: nc.vector.tensor_tensor(VHF[:TS, :], bk[0][:TS, 0:512], gt[:TS, hf * 512:(hf + 1) * 512], ALU.mult),
                           reads=[bk[1], gb_], writes=[bVHF])
                        op(DVE, lambda: nc.vector.tensor_tensor(xh, VHF[:TS, :], xh, ALU.add), reads=[bVHF, xb], writes=[xb])
                op(ACT, lambda: nc.scalar.activation(JUNK[:TS, :], xa, AF.Square, accum_out=SS2[:TS, t:t + 1]),
                   reads=[xb], writes=[bSS2], multi=True)
            op(POOL, lambda: nc.gpsimd.tensor_scalar(LN2[:TS, :NT], SS2[:TS, :NT], 1.0 / D, EPS, ALU.mult, ALU.add), reads=[bSS2], writes=[bRST2])
            op(POOL, lambda: nc.gpsimd.tensor_tensor(RST2[:TS, :NT], LN2[:TS, :NT], NHALF[:TS, :NT], ALU.pow), reads=[bRST2, bCONST], writes=[bRST2])
            for t in range(NT):
                xa, xb = c.xt[t]
                op(DVE, lambda: nc.vector.scalar_tensor_tensor(xa, xa, RST2[:TS, t:t + 1], GF[:TS, :], ALU.mult, ALU.mult),
                   reads=[xb, bRST2, bGF], writes=[xb])
                op(DVE, lambda: nc.vector.tensor_tensor(xa, xa, SHF[:TS, :], ALU.add), reads=[xb, bSHF], writes=[xb])
                for (dst, r0, r1) in c.yrows[t]:
                    dma(dst, xa[r0:r1, :], reads=[xb], dbuf=xb)

        def load_x(g):
            s = g % 2
            for t in range(4):
                r0 = g * 512 + t * 128
                dma(X[:, s, t, :], xp[r0:r0 + 128, :], writes=[bX[s][t]], dbuf=bX[s][t])

        if stop <= 3:
            return
        NG_ = NSEQ * 4
        blocks_p = [(i * 64, 64, "p") for i in range(8)]
        MKm_p = MKP[:, :]

        def pctx(g):
            s, b = g % 2, g // 4
            xt = [(X[:, s, t, :], bX[s][t]) for t in range(4)]
            yrows = [[(yp[g * 512 + t * 128: g * 512 + (t + 1) * 128, :], 0, 128)] for t in range(4)]
            return mkctx(xt, 512, 128, 4, [(0, 512, b)], blocks_p, 64, WMT, CG, MKm_p, yrows)

        def seq_pieces(b):
            def tail():
                dma(GF[:], modscr[b:b + 1, 1, :].partition_broadcast(128), reads=[bSCR], writes=[bGF], dbuf=bGF)
                dma(SHF[:], modscr[b:b + 1, 2, :].partition_broadcast(128), reads=[bSCR], writes=[bSHF], dbuf=bSHF)
            return restage_pieces(b) + [tail]

        load_x(0)
        cur = pctx(0)
        g_in(cur)
        for g in range(NG_):
            b = g // 4
            if g + 1 < NG_:
                load_x(g + 1)
            g_v1(cur)
            g_hA1(cur)
            g_v2(cur)
            g_hA2a(cur)
            g_hA2b(cur)
            g_gmlp(cur)
            g_hB(cur)
            nxt = pctx(g + 1) if g + 1 < NG_ else None
            fl = g_in_pieces(nxt) if nxt is not None else []
            if g % 4 == 0:
                sp_ = seq_pieces(b)
                mix = []
                while fl or sp_:
                    if sp_:
                        mix.append(sp_.pop(0))
                    if fl:
                        mix.append(fl.pop(0))
                    if fl:
                        mix.append(fl.pop(0))
                fl = mix
            g_hC(cur, fill=fl)
            if g % 4 == 3:
                dma(spo[b].rearrange("h d v -> d h v"), S["p"][:, :, :], reads=[bS["p"]], dbuf=bS["p"])
                op(POOL, lambda: nc.gpsimd.memset(S["p"][:], 0.0), writes=[bS["p"]])
            g_hD(cur)
            g_out(cur)
            if stop == 10 + g:
                dump('MIX', MIX[:], bMIX); dump('BM', BM[:], [bCONST]); dump('NHALF', NHALF[:], [bCONST]); dump('RSTD', RSTD[:], [bRSTD]); dump('DEC', DEC[:], [bDEC]); dump('RSV', RSV[:], [bRSV]); dump('RST2', RST2[:], [bRST2]); dump('SS', SS[:], [bSS]); dump('SS2', SS2[:], [bSS2]); dump('MV', MV[:], [bMV])
                return
            cur = nxt

        s = NG_ % 2
        xa, xb = X[0:64, s, 0, :], bX[s][0]
        op(POOL, lambda: nc.gpsimd.memset(xa, 0.0), writes=[xb])
        dma(X[0:16, s, 0, :], xs[0:16, :], writes=[xb], dbuf=xb)
        dma(X[32:48, s, 0, :], xs[16:32, :], writes=[xb], dbuf=xb)
        restage_wout(None)
        gt, gtb = X[0:64, s, 1, :], bX[s][1]
        dma(X[0:32, s, 1, :], modscr[4:5, 0, :].partition_broadcast(32), reads=[bSCR], writes=[gtb], dbuf=gtb)
        dma(X[32:64, s, 1, :], modscr[5:6, 0, :].partition_broadcast(32), reads=[bSCR], writes=[gtb], dbuf=gtb)
        LGROW = X[0:64, s, 2, 0:512]; LBROW = X[0:64, s, 2, 512:1024]; bLG = bX[s][2]
        dma(LGROW, lnvg_row.partition_broadcast(64), writes=[bLG], dbuf=bLG)
        dma(LBROW, lnvb_row.partition_broadcast(64), writes=[bLG], dbuf=bLG)
        VHF = X[0:64, s, 3, 0:512]; bVHF = bX[s][3]
        dma(GF[0:32, :], modscr[4:5, 1, :].partition_broadcast(32), reads=[bSCR], writes=[bGF], dbuf=bGF)
        dma(GF[32:64, :], modscr[5:6, 1, :].partition_broadcast(32), reads=[bSCR], writes=[bGF], dbuf=bGF)
        dma(SHF[0:32, :], modscr[4:5, 2, :].partition_broadcast(32), reads=[bSCR], writes=[bSHF], dbuf=bSHF)
        dma(SHF[32:64, :], modscr[5:6, 2, :].partition_broadcast(32), reads=[bSCR], writes=[bSHF], dbuf=bSHF)
        for i, key in enumerate(("s0", "s1")):
            S[key] = X[:, 1 - s, i, 0:512].rearrange("p (h v) -> p h v", h=4)
            bS[key] = bX[1 - s][i]
            dma(S[key], s0[i].rearrange("h d v -> d h v"), writes=[bS[key]], dbuf=bS[key])
        yrows = [[(ys[0:16, :], 0, 16), (ys[16:32, :], 32, 48)]]
        sc = mkctx([(xa, xb)], 64, 64, 1, [(0, 32, 4), (32, 64, 5)], [(0, 16, "s0"), (32, 16, "s1")], 32, WMTS, CGS, MKS[:, :],
                   yrows, gate_tile=(gt, gtb), sample=True)
        g_in(sc); g_v1(sc); g_hA1(sc); g_v2(sc); g_hA2a(sc); g_hA2b(sc); g_gmlp(sc); g_hB(sc); g_hC(sc); g_hD(sc); g_out(sc)
        for i, key in enumerate(("s0", "s1")):
            dma(sso[i].rearrange("h d v -> d h v"), S[key], reads=[bS[key]], dbuf=bS[key])

    def emit_all():
        emit_body()
        allb = [b_ for row in bX for b_ in row] + [bS["p"], bSCR, bGF, bSHF, bPARAM] + bSTG + dbg_bufs
        for b_ in allb:
            if b_.dsem is not None and SP.waited.get(b_.dsem.num, 0) < b_.dcnt:
                prog["sp"].append(lambda h, s_=b_.dsem, v_=b_.dcnt: h.wait_ge(s_, v_))
        for e in (PE, ACT, DVE, POOL):
            if e.cnt:
                prog["sp"].append(lambda h, s_=e.sem, v_=e.cnt: h.wait_ge(s_, v_))

    emit_all()

    @blk.sync
    def _(h):
        for th in prog["sp"]:
            th(h)

    @blk.tensor
    def _(h):
        for th in prog["pe"]:
            th(h)

    @blk.scalar
    def _(h):
        for th in prog["act"]:
            th(h)

    @blk.vector
    def _(h):
        for th in prog["dve"]:
            th(h)

    @blk.gpsimd
    def _(h):
        for th in prog["pool"]:
            th(h)

    es.close()
    return nc


_NC = None


def _consts():
    ident = np.eye(128, dtype=np.float32)
    s = np.arange(128)
    maskp = ((s[:, None] <= s[None, :]) & ((s[:, None] // 64) == (s[None, :] // 64))).astype(np.float32)
    q = np.arange(64)
    valid = (q % 32) < 16
    masks = ((q[:, None] <= q[None, :]) & ((q[:, None] // 32) == (q[None, :] // 32)) & valid[:, None] & valid[None, :]).astype(np.float32)
    cmask = ((s[:, None] // 64) <= (s[None, :] // 64)).astype(np.float32)
    rmask = np.zeros((128, 4), np.float32)
    rmask[:64, 0] = 1; rmask[64:, 1] = 1; rmask[0:16, 2] = 1; rmask[32:48, 3] = 1
    return ident, maskp, masks, cmask, rmask


def kernel(x_prompt, x_sample, c_prompt, c_sample, state_hgrn, norm_g, w_ada, b_ada, w_in, ln_v_g, ln_v_b,
           w_sp, b_sp, lb_logits, gnorm_g, w_out, g_final, w_ada_f, b_ada_f):
    global _NC
    f = lambda a: np.ascontiguousarray(np.asarray(a, dtype=np.float32))
    x_prompt, x_sample, c_prompt, c_sample, state_hgrn = map(f, (x_prompt, x_sample, c_prompt, c_sample, state_hgrn))
    if _NC is None:
        _NC = build()
    nc = _NC
    ident, maskp, masks, cmask, rmask = _consts()
    shared = {
        "ng": f(f(norm_g)[0].reshape(8, 128).T), "w_ada": f(f(w_ada)[0]), "b_ada": f(f(b_ada)[0][None]),
        "w_in": f(f(w_in)[0]), "lnvg": f(f(ln_v_g)[0].reshape(4, 128).T), "lnvg_row": f(f(ln_v_g)[0][None]),
        "lnvb_row": f(f(ln_v_b)[0][None]), "wspT": f(f(w_sp)[0].transpose(0, 2, 1)), "bsp": f(f(b_sp)[0][None]),
        "lbl": f(f(lb_logits).reshape(2, 4, 128).transpose(2, 0, 1)), "gn": f(f(gnorm_g)[0].reshape(4, 128).T),
        "w_out": f(f(w_out)[0]), "gfin": f(f(g_final)[None]), "w_adaf": f(w_ada_f), "b_adaf": f(f(b_ada_f)[None]),
        "ident": ident, "maskp": maskp, "masks": masks, "cmask": cmask, "rmask": rmask,
    }
    in_maps = []
    for i in range(8):
        m = dict(shared)
        m["xp"] = x_prompt[4 * i:4 * i + 4].reshape(NSEQ * L, D)
        m["xs"] = x_sample[2 * i:2 * i + 2].reshape(NSMP * LS, D)
        m["cT"] = f(np.concatenate([c_prompt[4 * i:4 * i + 4], c_sample[2 * i:2 * i + 2]], axis=0).T)
        m["s0"] = f(state_hgrn[0, 2 * i:2 * i + 2])
        in_maps.append(m)
    res = run_bass_kernel_spmd(nc, in_maps, core_ids=list(range(8)))
    r = res.results
    y_prompt = np.concatenate([r[i]["yp"].reshape(4, L, D) for i in range(8)], axis=0).astype(np.float32)
    y_sample = np.concatenate([r[i]["ys"].reshape(2, LS, D) for i in range(8)], axis=0).astype(np.float32)
    sp_ = np.concatenate([r[i]["spo"] for i in range(8)], axis=0)[None].astype(np.float32)
    ss_ = np.concatenate([r[i]["sso"] for i in range(8)], axis=0)[None].astype(np.float32)
    vs_ = np.concatenate([r[i]["vso"].reshape(2, LS, 512) for i in range(8)], axis=0)[None].astype(np.float32)
    return (y_prompt, y_sample, sp_, ss_, vs_)
```
